# Optimizing a Trainium2 kernel written in Bass

```python
import math
import jax, jax.numpy as jnp
from jax import lax
import numpy as np

D_MODEL = 1024
BATCH = 16
SEQ = 2048
DEPTH = 1
DEC_BATCH = 32
DEC_SEQ = 32
PAST_LEN = 4096

CHUNK = 64
QBLK = 128
PE_DIM = 256
DA_HEADS = 4
DA_HD = 64
DA_VD = 2 * DA_HD
RT_HEADS = 4
RT_KD = 128
RT_VD = 128
D_FF = 2816
CONV_W = 3
N_BUCKETS = 32
MAX_DIST = 128
EPS = 1e-6
ROPE_BASE = 10000.0

DA_Q = DA_HEADS * 2 * DA_HD
DA_K = DA_HEADS * 2 * DA_HD
DA_V = DA_HEADS * DA_VD
RT_Q = RT_HEADS * RT_KD
RT_K = RT_HEADS * RT_KD
RT_V = RT_HEADS * RT_VD
RT_G = RT_HEADS * RT_VD
IN_WIDTHS = (DA_Q, DA_K, DA_V, RT_Q, RT_K, RT_V, RT_G, D_MODEL, D_MODEL)
D_IN = DA_Q + DA_K + DA_V + RT_Q + RT_K + RT_V + RT_G + 2 * D_MODEL

kernel_name = "diffattn_retention_parallel_streaming_encoder"


def rmsnorm(x, g):
    xf = x.astype(jnp.float32)
    y = xf * lax.rsqrt(jnp.mean(xf * xf, axis=-1, keepdims=True) + EPS)
    return (y * g.astype(jnp.float32)).astype(x.dtype)


def t5_bucket(rel):
    nb = N_BUCKETS // 2
    ret = jnp.where(rel > 0, nb, 0)
    n = jnp.abs(rel)
    max_exact = nb // 2
    nf = jnp.maximum(n, 1).astype(jnp.float32)
    large = max_exact + (jnp.log(nf / max_exact) / math.log(MAX_DIST / max_exact) * (nb - max_exact)).astype(jnp.int32)
    large = jnp.minimum(large, nb - 1)
    return ret + jnp.where(n < max_exact, n, large)


def diff_attn_block(q1, q2, qpos, k1, k2, v, kpos, rel_bias, lam):
    bias = jnp.transpose(rel_bias[t5_bucket(kpos[None, :] - qpos[:, None])], (2, 0, 1)).astype(jnp.float32)
    mask = (kpos[None, :] // CHUNK) <= (qpos[:, None] // CHUNK)

    def probs(q, k):
        s = jnp.einsum('bqhd,bkhd->bhqk', q, k).astype(jnp.float32) + bias
        s = jnp.where(mask, s, -jnp.inf)
        return jax.nn.softmax(s, axis=-1)

    a = probs(q1, k1) - lam * probs(q2, k2)
    return jnp.einsum('bhqk,bkhe->bqhe', a.astype(v.dtype), v)


def diff_attention(q1, q2, qpos, k1, k2, v, kpos, rel_bias, lam):
    B, T, H, _ = q1.shape
    blk = min(QBLK, T)
    nb = T // blk

    def to_blocks(x):
        return x.reshape((B, nb, blk) + x.shape[2:]).swapaxes(0, 1)

    def one(args):
        bq1, bq2, bqp = args
        return diff_attn_block(bq1, bq2, bqp, k1, k2, v, kpos, rel_bias, lam)

    o = lax.map(one, (to_blocks(q1), to_blocks(q2), qpos.reshape(nb, blk)))
    return o.swapaxes(0, 1).reshape(B, T, H, v.shape[-1])


def rotary(x, pos):
    half = x.shape[-1] // 2
    inv = jnp.power(ROPE_BASE, -jnp.arange(half, dtype=jnp.float32) / half)
    ang = pos.astype(jnp.float32)[:, None] * inv[None, :]
    cos = jnp.cos(ang)[None, :, None, :]
    sin = jnp.sin(ang)[None, :, None, :]
    xf = x.astype(jnp.float32)
    x1, x2 = xf[..., :half], xf[..., half:]
    return jnp.concatenate([x1 * cos - x2 * sin, x1 * sin + x2 * cos], axis=-1).astype(x.dtype)


def retention(q, k, v, s0):
    B, T, H, dk = q.shape
    dv = v.shape[-1]
    c = min(CHUNK, T)
    n = T // c
    f32 = jnp.float32
    log_g = jnp.log1p(-jnp.power(2.0, -5.0 - jnp.arange(H, dtype=f32)))
    idx = jnp.arange(c, dtype=f32)
    diff = idx[:, None] - idx[None, :]
    dec_intra = jnp.where(diff >= 0, jnp.exp(log_g[:, None, None] * jnp.maximum(diff, 0.0)), 0.0)
    dec_q = jnp.exp(log_g[None, :] * (idx + 1.0)[:, None])
    dec_k = jnp.exp(log_g[None, :] * (c - 1.0 - idx)[:, None])
    dec_c = jnp.exp(log_g * c)

    def to_chunks(x):
        return x.astype(f32).reshape((B, n, c) + x.shape[2:]).swapaxes(0, 1)

    def step(S, inp):
        qi, ki, vi = inp
        att = jnp.einsum('bqhd,bkhd->bhqk', qi, ki) * dec_intra
        o = jnp.einsum('bhqk,bkhe->bqhe', att, vi)
        o = o + jnp.einsum('bqhd,bhde->bqhe', qi, S) * dec_q[None, :, :, None]
        S = S * dec_c[None, :, None, None] + jnp.einsum('bkhd,bkhe->bhde', ki * dec_k[None, :, :, None], vi)
        return S, o

    S, o = lax.scan(step, s0.astype(f32), (to_chunks(q), to_chunks(k), to_chunks(v)))
    return o.swapaxes(0, 1).reshape(B, T, H, dv), S.astype(s0.dtype)


def conv_ffn(f, s_conv, w_g, w_u, conv_w, conv_b, w_d):
    T = f.shape[1]
    g = f @ w_g
    u = f @ w_u
    gp = jnp.concatenate([s_conv.astype(g.dtype), g], axis=1)
    gc = conv_b + gp[:, 0:T] * conv_w[0]
    for j in range(1, CONV_W):
        gc = gc + gp[:, j:j + T] * conv_w[j]
    y = (jax.nn.gelu(gc) * u) @ w_d
    return y, gp[:, -(CONV_W - 1):]


def _layer(h, pe, cache_k, cache_v, s_ret, s_conv, rel_bias, lam_init,
           g_mix, w_in, g_q, g_k, lam_q1, lam_k1, lam_q2, lam_k2, g_da, g_rt,
           w_bd, w_br, w_o, g_ffn, w_g, w_u, conv_w, conv_b, w_d, g_pe, w_pe, w_pg):
    B, T, _ = h.shape
    past = 0 if cache_k is None else cache_k.shape[1]
    qpos = past + jnp.arange(T, dtype=jnp.int32)
    kpos = jnp.arange(past + T, dtype=jnp.int32)

    a = rmsnorm(h, g_mix)
    z = a @ w_in
    pts = np.cumsum(IN_WIDTHS)[:-1].tolist()
    qd, kd, vd, qr, kr, vr, gr, gate_d, gate_r = jnp.split(z, pts, axis=-1)

    q = rmsnorm(qd.reshape(B, T, DA_HEADS, 2, DA_HD), g_q) * (DA_HD ** -0.5)
    k = rmsnorm(kd.reshape(B, T, DA_HEADS, 2, DA_HD), g_k)
    v = vd.reshape(B, T, DA_HEADS, DA_VD)
    k_rows = k.reshape(B, T, DA_HEADS, 2 * DA_HD)
    if cache_k is None:
        k_all, v_all = k_rows, v
    else:
        k_all = jnp.concatenate([cache_k.astype(k_rows.dtype), k_rows], axis=1)
        v_all = jnp.concatenate([cache_v.astype(v.dtype), v], axis=1)
    lam = (jnp.exp(jnp.sum(lam_q1.astype(jnp.float32) * lam_k1.astype(jnp.float32)))
           - jnp.exp(jnp.sum(lam_q2.astype(jnp.float32) * lam_k2.astype(jnp.float32))) + lam_init)
    o_d = diff_attention(q[..., 0, :], q[..., 1, :], qpos, k_all[..., :DA_HD], k_all[..., DA_HD:],
                         v_all, kpos, rel_bias, lam)
    o_d = (rmsnorm(o_d, g_da) * (1.0 - lam_init)).reshape(B, T, DA_V)

    qr = rotary(qr.reshape(B, T, RT_HEADS, RT_KD), qpos)
    kr = rotary(kr.reshape(B, T, RT_HEADS, RT_KD), qpos) * (RT_KD ** -0.5)
    vr = vr.reshape(B, T, RT_HEADS, RT_VD)
    s0 = jnp.zeros((B, RT_HEADS, RT_KD, RT_VD), jnp.float32) if s_ret is None else s_ret
    o_r, s_new = retention(qr, kr, vr, s0)
    o_r = rmsnorm(o_r.astype(h.dtype), g_rt).reshape(B, T, RT_V) * jax.nn.silu(gr)

    mix = jax.nn.sigmoid(gate_d) * (o_d @ w_bd) + jax.nn.sigmoid(gate_r) * (o_r @ w_br)
    h = h + mix @ w_o

    sc = jnp.zeros((B, CONV_W - 1, D_FF), h.dtype) if s_conv is None else s_conv
    ffn_out, conv_new = conv_ffn(rmsnorm(h, g_ffn), sc, w_g, w_u, conv_w, conv_b, w_d)
    h = h + ffn_out

    h = h + (pe.astype(h.dtype) @ w_pe) * jax.nn.sigmoid(rmsnorm(h, g_pe) @ w_pg)
    return h, k_rows, v, s_new, conv_new


def setup_inputs(seed: int = 0) -> dict:
    key = jax.random.key(seed)
    ks = jax.random.split(key, 40)

    def nrm(k, shape, scale):
        return jax.random.normal(k, shape, jnp.float32) * scale

    def gain(k, shape):
        return 1.0 + 0.01 * jax.random.normal(k, shape, jnp.float32)

    L = DEPTH
    return {
        "x_prompt": nrm(ks[0], (BATCH, SEQ, D_MODEL), 1.0),
        "x_sample": nrm(ks[1], (DEC_BATCH, DEC_SEQ, D_MODEL), 1.0),
        "p_prompt": nrm(ks[2], (L, BATCH, SEQ, PE_DIM), 1.0),
        "p_sample": nrm(ks[3], (L, DEC_BATCH, DEC_SEQ, PE_DIM), 1.0),
        "cache_k": nrm(ks[4], (L, DEC_BATCH, PAST_LEN, DA_HEADS, 2 * DA_HD), 1.0),
        "cache_v": nrm(ks[5], (L, DEC_BATCH, PAST_LEN, DA_HEADS, DA_VD), 1.0),
        "state_ret": nrm(ks[6], (L, DEC_BATCH, RT_HEADS, RT_KD, RT_VD), 0.3),
        "state_conv": nrm(ks[7], (L, DEC_BATCH, CONV_W - 1, D_FF), 1.0),
        "rel_bias": nrm(ks[8], (N_BUCKETS, DA_HEADS), 0.5),
        "g_mix": gain(ks[9], (L, D_MODEL)),
        "w_in": nrm(ks[10], (L, D_MODEL, D_IN), D_MODEL ** -0.5),
        "g_q": gain(ks[11], (L, DA_HD)),
        "g_k": gain(ks[12], (L, DA_HD)),
        "lam_q1": nrm(ks[13], (L, DA_HD), 0.1),
        "lam_k1": nrm(ks[14], (L, DA_HD), 0.1),
        "lam_q2": nrm(ks[15], (L, DA_HD), 0.1),
        "lam_k2": nrm(ks[16], (L, DA_HD), 0.1),
        "g_da": gain(ks[17], (L, DA_VD)),
        "g_rt": gain(ks[18], (L, RT_VD)),
        "w_bd": nrm(ks[19], (L, DA_V, D_MODEL), DA_V ** -0.5),
        "w_br": nrm(ks[20], (L, RT_V, D_MODEL), RT_V ** -0.5),
        "w_o": nrm(ks[21], (L, D_MODEL, D_MODEL), D_MODEL ** -0.5),
        "g_ffn": gain(ks[22], (L, D_MODEL)),
        "w_g": nrm(ks[23], (L, D_MODEL, D_FF), D_MODEL ** -0.5),
        "w_u": nrm(ks[24], (L, D_MODEL, D_FF), D_MODEL ** -0.5),
        "conv_w": nrm(ks[25], (L, CONV_W, D_FF), CONV_W ** -0.5),
        "conv_b": nrm(ks[26], (L, D_FF), 0.01),
        "w_d": nrm(ks[27], (L, D_FF, D_MODEL), D_FF ** -0.5),
        "g_pe": gain(ks[28], (L, D_MODEL)),
        "w_pe": nrm(ks[29], (L, PE_DIM, D_MODEL), PE_DIM ** -0.5),
        "w_pg": nrm(ks[30], (L, D_MODEL, D_MODEL), D_MODEL ** -0.5),
    }


def reference(x_prompt, x_sample, p_prompt, p_sample, cache_k, cache_v, state_ret, state_conv,
              rel_bias, g_mix, w_in, g_q, g_k, lam_q1, lam_k1, lam_q2, lam_k2, g_da, g_rt,
              w_bd, w_br, w_o, g_ffn, w_g, w_u, conv_w, conv_b, w_d, g_pe, w_pe, w_pg):
    hp, hs = x_prompt, x_sample
    kps, vps, rps, cps = [], [], [], []
    kss, vss, rss, css = [], [], [], []
    for i in range(DEPTH):
        lam_init = 0.8 - 0.6 * math.exp(-0.3 * i)
        w = (g_mix[i], w_in[i], g_q[i], g_k[i], lam_q1[i], lam_k1[i], lam_q2[i], lam_k2[i],
             g_da[i], g_rt[i], w_bd[i], w_br[i], w_o[i], g_ffn[i], w_g[i], w_u[i],
             conv_w[i], conv_b[i], w_d[i], g_pe[i], w_pe[i], w_pg[i])
        hp, kp, vp, rp, cp = _layer(hp, p_prompt[i], None, None, None, None, rel_bias, lam_init, *w)
        hs, ksm, vsm, rsm, csm = _layer(hs, p_sample[i], cache_k[i], cache_v[i], state_ret[i],
                                        state_conv[i], rel_bias, lam_init, *w)
        kps.append(kp); vps.append(vp); rps.append(rp); cps.append(cp)
        kss.append(ksm); vss.append(vsm); rss.append(rsm); css.append(csm)
    k_prompt = jnp.stack(kps)
    v_prompt = jnp.stack(vps)
    ret_prompt = jnp.stack(rps)
    conv_prompt = jnp.stack(cps)
    k_sample = jnp.stack(kss)
    v_sample = jnp.stack(vss)
    ret_sample = jnp.stack(rss)
    conv_sample = jnp.stack(css)
    return (hp, hs, k_prompt, v_prompt, ret_prompt, conv_prompt, k_sample, v_sample, ret_sample, conv_sample)
```

```python
import math
from contextlib import ExitStack

import numpy as np
import concourse.bass as bass
import concourse.mybir as mybir
from concourse.bass_utils import run_bass_kernel_spmd

F32, BF16 = mybir.dt.float32, mybir.dt.bfloat16
AF = mybir.ActivationFunctionType
ALU = mybir.AluOpType
AX = mybir.AxisListType
ENGS = ("pe", "act", "dve", "pool", "sp")

D = 1024
DFF = 2816
NCC = 22
DIN = 5632
EPS = 1e-6
NEG = -30000.0


class Sched:
    EPOCH = 30000

    def __init__(self, nc, es):
        self.nc, self.es = nc, es
        self.ops = {e: [] for e in ENGS}
        self.esem = {e: None for e in ENGS}
        self.ecnt = {e: 0 for e in ENGS}
        self.last = {e: None for e in ENGS}
        self.dsem = {}
        self.res = {}
        self.seen = {e: {} for e in ENGS}
        self.nsem = 0
        self.pend = {e: [] for e in ENGS}

    def _newsem(self, name):
        self.nsem += 1
        return self.es.enter_context(self.nc.semaphore("s%d_%s" % (self.nsem, name)))

    def barrier(self):
        ticks = [t for t in self.last.values() if t is not None]
        ticks += [(d[0], d[1], None) for d in self.dsem.values() if d[1] > 0]
        for e in ENGS:
            self.pend[e] = list(ticks)

    def op(self, eng, fn, reads=(), writes=(), sig=True, dma_key=None):
        waits = {}

        def add(t, raw, force=False):
            sem, val, teng = t
            if not force and teng == eng and eng == "pe":
                return
            k = id(sem)
            if self.seen[eng].get(k, 0) >= val:
                return
            if k not in waits or waits[k][1] < val:
                waits[k] = (sem, val)

        for t in self.pend[eng]:
            add(t, True, force=(t[2] != eng))
        self.pend[eng] = []
        for r in reads:
            st = self.res.get(r)
            if st and st["w"]:
                add(st["w"], True)
        for w in writes:
            st = self.res.get(w)
            if st:
                if st["w"]:
                    add(st["w"], False)
                for t in st["r"]:
                    add(t, False)
        for k, (sem, val) in waits.items():
            self.seen[eng][k] = val
        inc = None
        if dma_key is not None:
            if dma_key not in self.dsem:
                self.dsem[dma_key] = [self._newsem("d"), 0]
            d = self.dsem[dma_key]
            d[1] += 16
            tick = (d[0], d[1], None)
            inc = (d[0], 16)
        else:
            if self.esem[eng] is None:
                self.esem[eng] = self._newsem(eng)
                self.ecnt[eng] = 0
            if sig:
                self.ecnt[eng] += 1
                tick = (self.esem[eng], self.ecnt[eng], eng)
                inc = (self.esem[eng], 1)
                if self.ecnt[eng] >= self.EPOCH:
                    self.esem[eng] = None
            else:
                tick = (self.esem[eng], self.ecnt[eng] + 1, eng)
            self.last[eng] = tick
        for r in reads:
            self.res.setdefault(r, {"w": None, "r": []})["r"].append(tick)
        for w in writes:
            self.res[w] = {"w": tick, "r": []}
        self.ops[eng].append((list(waits.values()), fn, inc))

    def emit(self):
        nc = self.nc
        final = [(d[0], d[1]) for d in self.dsem.values()]
        ops = self.ops
        with nc.Block() as block:
            def run(name, e, tail=False):
                for waits, fn, inc in ops[name]:
                    for sem, val in waits:
                        e.wait_ge(sem, val)
                    ins = fn(e)
                    if inc is not None:
                        ins.then_inc(inc[0], inc[1])
                if tail:
                    for sem, val in final:
                        e.wait_ge(sem, val)

            @block.tensor
            def _(e):
                run("pe", e)

            @block.scalar
            def _(e):
                run("act", e)

            @block.vector
            def _(e):
                run("dve", e)

            @block.gpsimd
            def _(e):
                run("pool", e)

            @block.sync
            def _(e):
                run("sp", e, tail=True)

    def dma(self, q, out, in_, key, reads=(), writes=()):
        self.op(q, lambda e: e.dma_start(out=out, in_=in_), reads, writes, dma_key=key)

    def mm(self, out, lhsT, rhs, start, stop, reads=(), writes=(), sig=None, **kw):
        if sig is None:
            sig = stop
        self.op("pe", lambda e: e.matmul(out, lhsT, rhs, start=start, stop=stop, **kw),
                reads, writes, sig=sig)

    def tr(self, out, in_, ident, reads=(), writes=(), sig=True):
        self.op("pe", lambda e: e.transpose(out, in_, ident), reads, writes, sig=sig)

    def act(self, out, in_, func, reads=(), writes=(), **kw):
        self.op("act", lambda e: e.activation(out, in_, func, **kw), reads, writes)

    def tt(self, eng, out, in0, in1, op, reads=(), writes=()):
        self.op(eng, lambda e: e.tensor_tensor(out, in0, in1, op), reads, writes)

    def ts(self, eng, out, in0, s1, s2, op0, op1, reads=(), writes=()):
        self.op(eng, lambda e: e.tensor_scalar(out, in0, s1, s2, op0, op1), reads, writes)

    def cp(self, eng, out, in_, reads=(), writes=()):
        if eng == "act":
            self.op(eng, lambda e: e.copy(out, in_), reads, writes)
        else:
            self.op(eng, lambda e: e.tensor_copy(out, in_), reads, writes)

    def red(self, eng, out, in_, reads=(), writes=()):
        self.op(eng, lambda e: e.tensor_reduce(out, in_, AX.X, ALU.add), reads, writes)

    def recip(self, out, in_, reads=(), writes=()):
        self.op("dve", lambda e: e.reciprocal(out, in_), reads, writes)

    def memset(self, eng, ap, val, reads=(), writes=()):
        self.op(eng, lambda e: e.memset(ap, val), reads, writes)


C_RMP, C_RMS, C_KM, C_GAM, C_DQK, NCST = 0, 128, 256, 260, 268, 284
C2_ID, C2_M0, C2_MN, C2_SMQ, NC2 = 0, 128, 256, 384, 896
V_GQ, V_GK, V_L, V_GDA, V_GRT, NV = 0, 64, 128, 384, 512, 640


def build(ntp=16, do_sample=True, do_p2=True):
    nc = bass.Bass("TRN2", target_bir_lowering=False)

    def din(name, shape):
        return nc.dram_tensor(name, list(shape), F32, kind="ExternalInput").ap()

    def dout(name, shape):
        return nc.dram_tensor(name, list(shape), F32, kind="ExternalOutput").ap()

    xp = din("xp", [2, 2048, D]); xs = din("xs", [128, D])
    pp = din("pp", [2, 2048, 256]); psm = din("psm", [128, 256])
    ck = din("ck", [4, 4096, 512]); cv = din("cv", [4, 4096, 512])
    sret = din("sret", [4, 4, 128, 128]); scv = din("scv", [128, NCC, 4, 2])
    w_in = din("w_in", [D, DIN]); w_bd = din("w_bd", [512, D]); w_br = din("w_br", [512, D])
    w_o = din("w_o", [D, D]); w_g = din("w_g", [D, DFF]); w_u = din("w_u", [D, DFF])
    w_d = din("w_d", [DFF, D]); w_pe = din("w_pe", [256, D]); w_pg = din("w_pg", [D, D])
    gfmd = din("gfm", [128, 3, 8]); gsm = din("gsm", [128, NV]); cw = din("cw", [128, 4, NCC])
    DR = din("DR", [128, 3, 4, 128]); CH = din("CH", [128, 4]); cst = din("cst", [128, NCST])
    cst2 = din("cst2", [128, NC2])
    rot = din("rot", [17, 128, 2, 128])
    yp = dout("yp", [2, 2048, D]); ys = dout("ys", [128, D])
    kp = dout("kp", [2, 2048, 512]); vp = dout("vp", [2, 2048, 512])
    rp = dout("rp", [2, 4, 128, 128]); cpo = dout("cpo", [2, 2, DFF])
    kso = dout("kso", [128, 512]); vso = dout("vso", [128, 512])
    rso = dout("rso", [4, 4, 128, 128]); cso = dout("cso", [4, 2, DFF])
    h1p = nc.dram_tensor("h1p", [2, 2048, D], F32, kind="Internal").ap()
    h1s = nc.dram_tensor("h1s", [128, D], F32, kind="Internal").ap()

    with ExitStack() as es:
        S = Sched(nc, es)
        PS = es.enter_context(nc.psum_tensor("PS", [128, 4096], F32))

        def bank(k):
            return PS[:, k * 512:(k + 1) * 512]

        def bankb(k):
            return PS[:, k * 512:(k + 1) * 512].bitcast(BF16)

        def pb(k):
            return ("ps", k)

        gpc = [0]

        def gp():
            gpc[0] ^= 1
            return gpc[0]

        def sbt(st, name, shape, dt):
            return st.enter_context(nc.sbuf_tensor(name, list(shape), dt))

        cst_sb = sbt(es, "cst_sb", [128, NCST], F32)
        idb = sbt(es, "idb", [128, 128], BF16)
        gfm = sbt(es, "gfm_sb", [128, 3, 8], F32)
        gs = sbt(es, "gs", [128, NV], F32)
        cw_sb = sbt(es, "cw_sb", [128, 4, NCC], F32)
        lam = sbt(es, "lam", [128, 8], F32)
        S.dma("sp", cst_sb[:], cst, "c0", writes=["cst"])
        S.dma("pool", idb[:], cst2[:, C2_ID:C2_ID + 128], "c1", writes=["idb"])
        S.dma("sp", gfm[:], gfmd, "c2", writes=["gfm"])
        S.dma("sp", gs[:], gsm, "c3", writes=["gs"])
        S.dma("sp", cw_sb[:], cw, "c4", writes=["cw"])

        def rms_rstd(ss_ap, n, dim, tagr, tagw, tmp_ap, out_ap):
            S.ts("dve", tmp_ap, ss_ap, 1.0 / dim, EPS, ALU.mult, ALU.add, reads=[tagr], writes=[tagw + "_a"])
            S.act(tmp_ap, tmp_ap, AF.Ln, reads=[tagw + "_a"], writes=[tagw + "_b"])
            S.act(out_ap, tmp_ap, AF.Exp, reads=[tagw + "_b"], writes=[tagw], scale=-0.5)

        with ExitStack() as e1:
            w_in_sb = sbt(e1, "w_in_sb", [128, 8, DIN], BF16)
            w_bd_sb = sbt(e1, "w_bd_sb", [128, 4, D], BF16)
            w_br_sb = sbt(e1, "w_br_sb", [128, 4, D], BF16)
            w_o_sb = sbt(e1, "w_o_sb", [128, 8, D], BF16)
            def v4(ap):
                return ap.rearrange("p (a b) -> p a b", a=4)

            w_in_v = w_in.rearrange("(k p) n -> p k n", p=128)
            for k in range(8):
                S.dma("pool", w_in_sb[:, k, :], w_in_v[:, k, :], ("w_in", k), writes=[("w_in", k)])
            S.dma("pool", w_bd_sb[:], w_bd.rearrange("(k p) n -> p k n", p=128), "w_bd", writes=["w_bd"])
            S.dma("pool", w_br_sb[:], w_br.rearrange("(k p) n -> p k n", p=128), "w_br", writes=["w_br"])
            S.dma("pool", w_o_sb[:], w_o.rearrange("(k p) n -> p k n", p=128), "w_o", writes=["w_o"])
            W_IN = [("w_in", k) for k in range(8)]

            T3 = sbt(e1, "T3", [128, 3, 512], F32)
            tA, tB, tC = T3[:, 0, :], T3[:, 1, :], T3[:, 2, :]
            DRs = T3[:].rearrange("p k (h q) -> p k h q", h=4)
            vrow_t = sbt(e1, "vrow", [128, 512], F32)
            od = vrow_t
            sg = sbt(e1, "sg", [128, 512], F32)
            CHs = sbt(e1, "CHs", [128, 4], F32)
            DH = sbt(e1, "DH", [128, 3, 4, 128], BF16)
            DL = sbt(e1, "DL", [128, 3, 4, 128], BF16)
            smq = sbt(e1, "smq", [128, 4, 128], BF16)
            S.dma("sp", DRs, DR, "c5", writes=["DRs"])
            S.dma("sp", CHs[:], CH, "c6", writes=["CHs"])
            S.dma("sp", sg[:, 0:256], cst2[:, C2_M0:C2_M0 + 256], "c8", writes=["msk"])
            S.dma("pool", smq[:], cst2[:, C2_SMQ:C2_SMQ + 512].rearrange("p (b q) -> p b q", b=4), "c7", writes=["smq"])
            for kd in range(3):
                S.tt("dve", DRs[:, kd], DRs[:, kd], CHs[:].unsqueeze(2).to_broadcast([128, 4, 128]), ALU.subtract,
                     reads=["DRs", "CHs"], writes=["DRs"])
            for kd, off in ((0, 0), (2, 128)):
                S.tt("dve", DRs[:, kd], DRs[:, kd], sg[:, off:off + 128].unsqueeze(1).to_broadcast([128, 4, 128]),
                     ALU.add, reads=["DRs", "msk"], writes=["DRs"])
            S.cp("dve", DH[:], DRs, reads=["DRs"], writes=["DH"])
            for kd in range(3):
                S.tt("dve", v4(od[:]), DRs[:, kd], DH[:, kd], ALU.subtract, reads=["DRs", "DH"], writes=["odtmp"])
                S.cp("dve", DL[:, kd], v4(od[:]), reads=["odtmp"], writes=["DL"])
            junk64 = sbt(e1, "junk64", [128, 2, 64], F32)
            gl = gs[:, V_L:V_L + 256].rearrange("p (a b c) -> p a b c", a=2, b=2)
            S.tt("dve", junk64[:], gl[:, :, 0, :], gl[:, :, 1, :], ALU.mult, reads=["gs"], writes=["junk64"])
            S.red("dve", lam[:, 0:2], junk64[:], reads=["junk64"], writes=["lam0"])
            S.act(lam[:, 4:6], lam[:, 0:2], AF.Exp, reads=["lam0"], writes=["lam1"])
            S.tt("dve", lam[:, 2:3], lam[:, 5:6], lam[:, 4:5], ALU.subtract, reads=["lam1"], writes=["lam2"])
            S.ts("dve", lam[:, 3:4], lam[:, 2:3], 1.0, -0.2, ALU.mult, ALU.add, reads=["lam2"], writes=["lam"])
            NLAM = lam[:, 3:4]

            ARENA = sbt(e1, "ARENA", [128, 9032], F32)
            AB = ARENA[:].bitcast(BF16)
            KT = AB[:, 0:8192].rearrange("p (h t) -> p h t", h=4)
            VC = AB[:, 8192:16512].rearrange("p (j h e) -> p j h e", j=16, h=4)
            Sst_p = ARENA[:, 8256:8768].rearrange("p (b h e) -> p b h e", b=1, h=4)
            Sbf_p = AB[:, 17536:18048].rearrange("p (b h e) -> p b h e", b=1, h=4)
            Kt = [AB[:, i * 1024:(i + 1) * 1024].rearrange("p (t f) -> p t f", t=2) for i in range(2)]
            Vt = [AB[:, 2048 + i * 1040:2048 + (i + 1) * 1040].rearrange("p (t h e) -> p t h e", t=2, h=4) for i in range(2)]
            KTc = [AB[:, 4128 + i * 1024:4128 + (i + 1) * 1024].rearrange("p (a h t) -> p a h t", a=2, h=4) for i in range(2)]
            qm = AB[:, 6176:8224].rearrange("p (b h t) -> p b h t", b=4, h=4)
            km = AB[:, 8224:10272].rearrange("p (b h t) -> p b h t", b=4, h=4)
            KTs = AB[:, 10272:10784].rearrange("p (h t) -> p h t", h=4)
            VS = AB[:, 10784:11304].rearrange("p (h e) -> p h e", h=4)
            Sbf_s = AB[:, 11304:13352].rearrange("p (b h e) -> p b h e", b=4, h=4)
            Sst_s = ARENA[:, 6676:8724].rearrange("p (b h e) -> p b h e", b=4, h=4)
            xt = [sbt(e1, "xt%d" % i, [128, D], F32) for i in range(2)]
            rt = [sbt(e1, "rt%d" % i, [128, 2, 128], F32) for i in range(2)]
            oT = sbt(e1, "oT", [128, 8, 128], BF16)
            a_bf = oT[:].rearrange("p k t -> p (k t)")
            aT = sbt(e1, "aT", [128, 8, 128], BF16)
            st = sbt(e1, "st", [128, 64], F32)
            krow_t = sbt(e1, "krow", [128, 512], F32)
            krow, vrow, sgd, sgr = krow_t[:], vrow_t[:], krow_t[:], vrow_t[:]
            od_bf_t = sbt(e1, "od_bf", [128, 512], BF16)
            or_bf_t = sbt(e1, "or_bf", [128, 512], BF16)
            od_bf, or_bf, q_bf, k_bf = od_bf_t[:], or_bf_t[:], od_bf_t[:], or_bf_t[:]
            attb = sbt(e1, "attb", [128, 4, 128], BF16)
            qt_bf = attb
            kt_bf = sbt(e1, "kt_bf", [128, 4, 128], BF16)
            qtT = sbt(e1, "qtT", [128, 4, 128], BF16)
            ktT = sbt(e1, "ktT", [128, 4, 128], BF16)
            vr_bf = sbt(e1, "vr_bf", [128, 512], BF16)
            PT = [sbt(e1, "PT%d" % i, [128, 512], BF16) for i in range(3)]
            mixT = sbt(e1, "mixT", [128, 8, 128], BF16)
            junk = mixT[:].rearrange("p k t -> p (k t)")
            qTz = mixT[:].rearrange("p (m h) t -> p m h t", m=2)
            MX = [("mixT", 0), ("mixT", 1)]

            S.ts("dve", gs[:, V_GQ:V_GQ + 64], gs[:, V_GQ:V_GQ + 64], 0.125, 0.0, ALU.mult, ALU.add, reads=["gs"], writes=["gs"])
            S.ts("dve", gs[:, V_GDA:V_GDA + 128], gs[:, V_GDA:V_GDA + 128], 0.8, 0.0, ALU.mult, ALU.add, reads=["gs"], writes=["gs"])
            S.barrier()
            S.memset("pool", VS[:, :, 128:129], 1.0, writes=["VSones"])
            for i in range(2):
                S.memset("pool", Vt[i][:, :, :, 128:129], 1.0, writes=[("Vtones", i)])

            ptc = [0]
            stc = [0]

            def p1_load(kind, s, i, tix):
                sl = tix % 2
                if kind == "s":
                    x_ap, rot_ap = xs, rot[16]
                else:
                    x_ap, rot_ap = xp[s, i * 128:(i + 1) * 128, :], rot[i]
                S.dma("sp", xt[sl][:], x_ap, ("x", sl), writes=[("x", sl)])
                S.dma("sp", rt[sl][:], rot_ap, ("rt", sl), writes=[("rt", sl)])

            fronted = set()

            stated = set()

            def p1_stats(kind, s, i, tix):
                sl = tix % 2
                X = ("x", sl)
                stated.add(tix)
                S.act(od_bf[:], xt[sl][:, 0:512], AF.Square, accum_out=st[:, 0:1], reads=[X], writes=["od_bf", "ss0a"])
                S.act(or_bf[:], xt[sl][:, 512:1024], AF.Square, accum_out=st[:, 3:4], reads=[X], writes=["or_bf", "ss0b"])
                S.tt("dve", st[:, 0:1], st[:, 0:1], st[:, 3:4], ALU.add, reads=["ss0a", "ss0b"], writes=["ss0"])
                S.ts("dve", st[:, 1:2], st[:, 0:1], 1.0 / D, EPS, ALU.mult, ALU.add, reads=["ss0"], writes=["r0_a"])

            def p1_front_a(kind, s, i, tix):
                sl = tix % 2
                X = ("x", sl)
                if tix not in stated:
                    p1_stats(kind, s, i, tix)
                fronted.add(tix)
                S.act(st[:, 1:2], st[:, 1:2], AF.Ln, reads=["r0_a"], writes=["r0_b"])
                S.act(st[:, 2:3], st[:, 1:2], AF.Exp, reads=["r0_b"], writes=["r0"], scale=-0.5)
                S.op("dve", lambda e: e.tensor_scalar_mul(a_bf[:], xt[sl][:], st[:, 2:3]),
                     reads=[X, "r0"], writes=["oT"])

            def p1_tile(kind, s, i, tix, nxt):
                sl = tix % 2
                samp = kind == "s"
                if samp:
                    x_ap, rot_ap = xs, rot[16]
                    k_out, v_out, h_out = kso, vso, h1s
                else:
                    rows = slice(i * 128, (i + 1) * 128)
                    x_ap, rot_ap = xp[s, rows, :], rot[i]
                    k_out, v_out, h_out = kp[s, rows, :], vp[s, rows, :], h1p[s, rows, :]
                X, RT = ("x", sl), ("rt", sl)
                if nxt is not None:
                    p1_load(*nxt)
                if tix not in fronted:
                    p1_front_a(kind, s, i, tix)
                b = gp()
                for k in range(8):
                    S.tr(bankb(b)[:, k * 128:(k + 1) * 128], a_bf[:, k * 128:(k + 1) * 128], idb[:],
                         reads=["oT", "idb"], writes=[pb(b)], sig=(k == 7))
                S.tt("dve", aT[:], bankb(b)[:, 0:1024].rearrange("p (k t) -> p k t", k=8),
                     gfm[:, 0, :].unsqueeze(2).to_broadcast([128, 8, 128]), ALU.mult,
                     reads=[pb(b), "gfm"], writes=["aT"])

                ZB = [6, 7, 2, 3, 4, 5]

                def zchunk(c):
                    b = ZB[c] if c < 6 else gp()
                    for k in range(8):
                        S.mm(bank(b), aT[:, k, :], w_in_sb[:, k, c * 512:(c + 1) * 512], k == 0, k == 7,
                             reads=["aT", W_IN[k]], writes=[pb(b)])
                    return b

                def qknorm(b, goff, tag):
                    S.act(tA[:], bank(b), AF.Square, reads=[pb(b)], writes=["tA"])
                    S.red("dve", st[:, 8:16], tA[:].rearrange("p (a b) -> p a b", a=8), reads=["tA"], writes=["ss8"])
                    rms_rstd(st[:, 8:16], 8, 64, "ss8", "r8", st[:, 16:24], st[:, 24:32])
                    S.tt("dve", tB[:].rearrange("p (a b) -> p a b", a=8), bank(b).rearrange("p (a b) -> p a b", a=8),
                         st[:, 24:32].unsqueeze(2).to_broadcast([128, 8, 64]), ALU.mult,
                         reads=[pb(b), "r8"], writes=["tB"])

                zb = [zchunk(c) for c in range(6)]
                b = zb[0]
                qknorm(b, V_GQ, "q")
                S.tt("dve", q_bf[:].rearrange("p (a b) -> p a b", a=8), tB[:].rearrange("p (a b) -> p a b", a=8),
                     gs[:, V_GQ:V_GQ + 64].unsqueeze(1).to_broadcast([128, 8, 64]), ALU.mult,
                     reads=["tB", "gs"], writes=["od_bf"])
                b = zb[1]
                qknorm(b, V_GK, "k")
                S.tt("dve", krow[:].rearrange("p (a b) -> p a b", a=8), tB[:].rearrange("p (a b) -> p a b", a=8),
                     gs[:, V_GK:V_GK + 64].unsqueeze(1).to_broadcast([128, 8, 64]), ALU.mult,
                     reads=["tB", "gs"], writes=["krow"])
                S.dma("sp", k_out, krow[:], "ko", reads=["krow"])
                S.cp("pool", k_bf[:], krow[:], reads=["krow"], writes=["or_bf"])
                b = gp()
                for h in range(4):
                    S.tr(bankb(b)[:, h * 128:(h + 1) * 128], q_bf[:, h * 128:(h + 1) * 128], idb[:],
                         reads=["od_bf", "idb"], writes=[pb(b)], sig=False)
                for h in range(4):
                    S.tr(bankb(b)[:, 512 + h * 128:512 + (h + 1) * 128], k_bf[:, h * 128:(h + 1) * 128], idb[:],
                         reads=["or_bf", "idb"], writes=[pb(b)], sig=(h == 3))
                S.memset("pool", mixT[:], 0.0, writes=MX)
                S.cp("act", qTz[0:64, 0], v4(bankb(b)[0:64, 0:512]), reads=[pb(b)] + MX, writes=MX)
                S.cp("act", qTz[64:128, 1], v4(bankb(b)[64:128, 0:512]), reads=[pb(b)] + MX, writes=MX)
                if samp:
                    S.cp("act", KTs[:].rearrange("p h t -> p (h t)"), bankb(b)[:, 512:1024], reads=[pb(b)], writes=["KTs"])
                else:
                    S.cp("act", KT[:, :, i * 128:(i + 1) * 128], v4(bankb(b)[:, 512:1024]), reads=[pb(b)], writes=[("KT", i)])
                b = zb[2]
                S.cp("act", vrow[:], bank(b), reads=[pb(b)], writes=["vrow"])
                S.dma("sp", v_out, vrow[:], "vo", reads=["vrow"])
                if samp:
                    S.cp("pool", VS[:, :, 0:128], v4(vrow[:]), reads=["vrow", "VSones"], writes=["VS"])
                else:
                    S.cp("pool", VC[:, i, :, 0:128], v4(vrow[:]), reads=["vrow", "VCones"], writes=[("VC", i)])

                def rotary(c, qk, out_bf, tagout):
                    b = zb[c]
                    dq0 = C_DQK + (8 if samp else 0) + qk * 4
                    S.tt("dve", v4(tA[:]), v4(bank(b)), cst_sb[:, dq0:dq0 + 4].unsqueeze(2).to_broadcast([128, 4, 128]), ALU.mult,
                         reads=[pb(b), "cst"], writes=["tA"])
                    CC = rt[sl][:, 0, :].unsqueeze(1).to_broadcast([128, 4, 128])
                    S.tt("pool", v4(tB[:]), v4(tA[:]), CC, ALU.mult, reads=["tA", RT], writes=["tB"])
                    S.tt("pool", v4(tC[:])[:, :, 0:64], v4(tA[:])[:, :, 64:128],
                         rt[sl][:, 1, 0:64].unsqueeze(1).to_broadcast([128, 4, 64]), ALU.mult,
                         reads=["tA", RT], writes=["tC0"])
                    S.tt("pool", v4(tC[:])[:, :, 64:128], v4(tA[:])[:, :, 0:64],
                         rt[sl][:, 1, 64:128].unsqueeze(1).to_broadcast([128, 4, 64]), ALU.mult,
                         reads=["tA", RT], writes=["tC1"])
                    S.tt("dve", out_bf[:].rearrange("p h d -> p (h d)"), tB[:], tC[:], ALU.add,
                         reads=["tB", "tC0", "tC1"], writes=[tagout])

                rotary(3, 0, qt_bf, "attb")
                rotary(4, 1, kt_bf, "kt_bf")
                b = zb[5]
                S.cp("act", vr_bf[:], bank(b), reads=[pb(b)], writes=["vr_bf"])
                b = zchunk(6)
                S.act(sg[:], bank(b), AF.Silu, reads=[pb(b)], writes=["sg"])

                OB = [2, 3, 4, 5]
                if not samp:
                    groups = []
                    for h in range(4):
                        js = list(range(i + 1))
                        for g0 in range(0, len(js), 2):
                            groups.append((h, js[g0:g0 + 2]))

                    def emit_st(g):
                        h, js = g
                        sb_ = 6 + (stc[0] % 2)
                        stc[0] += 1
                        for jj, j in enumerate(js):
                            band = j >= i - 1
                            kd = 0 if j == i else 1
                            o = bank(sb_)[:, jj * 256:(jj + 1) * 256].rearrange("p (m t) -> p m t", m=2)
                            S.mm(o, KT[:, h, j * 128:(j + 1) * 128], qTz[:, :, h, :],
                                 True, not band, reads=[("KT", j)] + MX, writes=[pb(sb_)])
                            if band:
                                S.mm(o, idb[:], DH[:, kd, h, :].unsqueeze(1).to_broadcast([128, 2, 128]), False, False,
                                     reads=["idb", "DH"], writes=[pb(sb_)])
                                S.mm(o, idb[:], DL[:, kd, h, :].unsqueeze(1).to_broadcast([128, 2, 128]), False, True,
                                     reads=["idb", "DL"], writes=[pb(sb_)])
                        return sb_

                    def emit_exp_pv(g, sb_):
                        h, js = g
                        w = len(js) * 256
                        pt = ptc[0] % 3
                        ptc[0] += 1
                        S.act(PT[pt][:, 0:w], bank(sb_)[:, 0:w], AF.Exp, reads=[pb(sb_)], writes=[("PT", pt)])
                        for jj, j in enumerate(js):
                            for m in range(2):
                                S.mm(bank(OB[h])[:, m * 129:(m + 1) * 129], PT[pt][:, (jj * 2 + m) * 128:(jj * 2 + m + 1) * 128],
                                     VC[:, j, h, 0:129], (j == 0 and m == 0), j == i,
                                     reads=[("PT", pt), ("VC", j), "VCones"], writes=[pb(OB[h])],
                                     sig=(j == i and m == 1), skip_group_check=True)

                    prev = emit_st(groups[0])
                    for gi, g in enumerate(groups):
                        nxt = emit_st(groups[gi + 1]) if gi + 1 < len(groups) else None
                        emit_exp_pv(g, prev)
                        prev = nxt
                else:
                    for hp in range(2):
                        sb_ = 6 + (stc[0] % 2)
                        stc[0] += 1
                        for hh in range(2):
                            h = hp * 2 + hh
                            o = bank(sb_)[:, hh * 256:(hh + 1) * 256].rearrange("p (m t) -> p m t", m=2)
                            S.mm(o, KTs[:, h, :], qTz[:, :, h, :], True, False,
                                 reads=["KTs"] + MX, writes=[pb(sb_)])
                            S.mm(o, idb[:], DH[:, 2, h, :].unsqueeze(1).to_broadcast([128, 2, 128]), False, False,
                                 reads=["idb", "DH"], writes=[pb(sb_)])
                            S.mm(o, idb[:], DL[:, 2, h, :].unsqueeze(1).to_broadcast([128, 2, 128]), False, True,
                                 reads=["idb", "DL"], writes=[pb(sb_)])
                        pt = ptc[0] % 3
                        ptc[0] += 1
                        S.act(PT[pt][:], bank(sb_), AF.Exp, reads=[pb(sb_)], writes=[("PT", pt)])
                        for hh in range(2):
                            h = hp * 2 + hh
                            for m in range(2):
                                S.mm(bank(OB[h])[:, m * 129:(m + 1) * 129], PT[pt][:, (hh * 2 + m) * 128:(hh * 2 + m + 1) * 128],
                                     VS[:, h, 0:129], m == 0, False, reads=[("PT", pt), "VS", "VSones"], writes=[pb(OB[h])],
                                     sig=True, skip_group_check=True)
                    ldc = [0]
                    for bq in range(4 if "cache" not in _SKIP else 0):
                        for j2 in range(16):
                            ls = ldc[0] % 2
                            ldc[0] += 1
                            S.dma("pool", Kt[ls], ck[bq, j2 * 256:(j2 + 1) * 256, :].rearrange("(t p) f -> p t f", p=128),
                                  ("Kt", ls), writes=[("Kt", ls)])
                            for t_ in range(2):
                                S.dma("pool", Vt[ls][:, t_, :, 0:128],
                                      cv[bq, j2 * 256 + t_ * 128:j2 * 256 + (t_ + 1) * 128, :].rearrange("p (h e) -> p h e", h=4),
                                      ("Vt", ls, t_), reads=[("Vtones", ls)], writes=[("Vt", ls, t_)])
                            ks_ = ls
                            if "cpe" in _SKIP:
                                continue
                            b = gp()
                            for jj in range(2):
                                for h in range(4):
                                    S.tr(bankb(b)[:, (jj * 4 + h) * 128:(jj * 4 + h + 1) * 128], Kt[ls][:, jj, h * 128:(h + 1) * 128],
                                         idb[:], reads=[("Kt", ls), "idb"], writes=[pb(b)], sig=(jj == 1 and h == 3))
                            S.cp("act", KTc[ks_].rearrange("p a h t -> p (a h t)"), bankb(b)[:, 0:1024],
                                 reads=[pb(b)], writes=[("KTc", ks_)])
                            if "cst" in _SKIP:
                                continue
                            sb_ = 6 + (stc[0] % 2)
                            stc[0] += 1
                            for jj in range(2):
                                j = j2 * 2 + jj
                                band = j == 31
                                for h in range(4):
                                    c0 = (jj * 4 + h) * 64
                                    o = bank(sb_)[:, c0:c0 + 64].rearrange("p (m t) -> p m t", m=2)
                                    S.mm(o, KTc[ks_][:, jj, h, :], qTz[:, :, h, bq * 32:(bq + 1) * 32],
                                         True, not band, reads=[("KTc", ks_)] + MX, writes=[pb(sb_)],
                                         sig=(not band and jj == 1 and h == 3))
                                    if band:
                                        S.mm(o, idb[:], DH[:, 1, h, 0:32].unsqueeze(1).to_broadcast([128, 2, 32]), False, False,
                                             reads=["idb", "DH"], writes=[pb(sb_)])
                                        S.mm(o, idb[:], DL[:, 1, h, 0:32].unsqueeze(1).to_broadcast([128, 2, 32]), False, True,
                                             reads=["idb", "DL"], writes=[pb(sb_)])
                            pt = ptc[0] % 3
                            ptc[0] += 1
                            S.act(PT[pt][:], bank(sb_), AF.Exp, reads=[pb(sb_)], writes=[("PT", pt)])
                            for jj in range(2):
                                j = j2 * 2 + jj
                                for h in range(4):
                                    for m in range(2):
                                        c0 = ((jj * 4 + h) * 2 + m) * 32
                                        S.mm(bank(OB[h])[bq * 32:(bq + 1) * 32, m * 129:(m + 1) * 129], PT[pt][:, c0:c0 + 32],
                                             Vt[ls][:, jj, h, 0:129], False, j == 31,
                                             reads=[("PT", pt), ("Vt", ls, 0), ("Vt", ls, 1), ("Vtones", ls)], writes=[pb(OB[h])],
                                             sig=(jj == 1 and h == 3 and m == 1), skip_group_check=True,
                                             tile_position=(0, bq * 32))
                b = gp()
                for h in range(4):
                    S.tr(bankb(b)[:, h * 128:(h + 1) * 128], qt_bf[:, h, :], idb[:],
                         reads=["attb", "idb"], writes=[pb(b)], sig=False)
                for h in range(4):
                    S.tr(bankb(b)[:, 512 + h * 128:512 + (h + 1) * 128], kt_bf[:, h, :], idb[:],
                         reads=["kt_bf", "idb"], writes=[pb(b)], sig=(h == 3))
                S.cp("act", qtT[:].rearrange("p h t -> p (h t)"), bankb(b)[:, 0:512], reads=[pb(b)], writes=["qtT"])
                S.cp("act", ktT[:].rearrange("p h t -> p (h t)"), bankb(b)[:, 512:1024], reads=[pb(b)], writes=["ktT"])
                Ov = PS[:, 1024:3072].rearrange("p (h x) -> p h x", h=4)
                OBR = [pb(k) for k in OB]
                Osum = Ov[:, :, 0:258].rearrange("p h (m e) -> p h m e", m=2)[:, :, :, 128]
                S.recip(st[:, 32:40].rearrange("p (h m) -> p h m", h=4), Osum, reads=OBR, writes=["rec"])
                recv = st[:, 32:40].rearrange("p (h m) -> p h m", h=4)
                S.tt("dve", st[:, 40:44], recv[:, :, 1], NLAM.to_broadcast([128, 4]), ALU.mult, reads=["rec", "lam"], writes=["rec2"])
                S.tt("dve", v4(tA[:]), Ov[:, :, 0:128], recv[:, :, 0:1].to_broadcast([128, 4, 128]), ALU.mult,
                     reads=OBR + ["rec"], writes=["tA"])
                S.tt("dve", v4(tB[:]), Ov[:, :, 129:257], st[:, 40:44].unsqueeze(2).to_broadcast([128, 4, 128]), ALU.mult,
                     reads=OBR + ["rec2"], writes=["tB"])
                S.tt("pool", od[:], tA[:], tB[:], ALU.add, reads=["tA", "tB"], writes=["vrow"])
                S.act(tC[:], od[:], AF.Square, reads=["vrow"], writes=["tC0", "tC1"])
                S.red("dve", st[:, 44:48], v4(tC[:]), reads=["tC0", "tC1"], writes=["ss4"])
                rms_rstd(st[:, 44:48], 4, 128, "ss4", "r4", st[:, 48:52], st[:, 52:56])
                S.tt("dve", v4(tA[:]), v4(od[:]), st[:, 52:56].unsqueeze(2).to_broadcast([128, 4, 128]), ALU.mult,
                     reads=["vrow", "r4"], writes=["tA"])
                S.tt("pool", v4(od_bf[:]), v4(tA[:]), gs[:, V_GDA:V_GDA + 128].unsqueeze(1).to_broadcast([128, 4, 128]), ALU.mult,
                     reads=["tA", "gs"], writes=["od_bf"])

                nsq = 4 if samp else 1
                first = samp or i == 0
                gam = cst_sb[:, C_GAM + (4 if samp else 0):C_GAM + (4 if samp else 0) + 4]
                rmoff = C_RMS if samp else C_RMP
                Sst = Sst_s if samp else Sst_p
                Sbf = Sbf_s if samp else Sbf_p
                if samp:
                    if "sret" not in _SKIP:
                        S.dma("sp", Sst, sret.rearrange("b h d e -> d b h e"), "sst", writes=["Sst"])
                        S.dma("pool", Sbf, sret.rearrange("b h d e -> d b h e"), "sbf", writes=["Sbf"])
                    for bq in range(4):
                        S.tt("dve", qm[:, bq], qtT[:], smq[:, bq, :].unsqueeze(1).to_broadcast([128, 4, 128]), ALU.mult,
                             reads=["qtT", "smq"], writes=["qm"])
                        S.tt("dve", km[:, bq].rearrange("p h d -> p (h d)"), kt_bf[:].rearrange("p h d -> p (h d)"),
                             cst_sb[:, C_KM + bq:C_KM + bq + 1].to_broadcast([128, 512]), ALU.mult,
                             reads=["kt_bf", "cst"], writes=[("km", bq)])
                b1 = gp()
                for h in range(4):
                    S.mm(bank(b1)[:, h * 128:(h + 1) * 128], ktT[:, h, :], qtT[:, h, :], True, True,
                         reads=["ktT", "qtT"], writes=[pb(b1)], sig=(h == 3))
                S.tt("dve", attb[:], v4(bank(b1)), cst_sb[:, rmoff:rmoff + 128].unsqueeze(1).to_broadcast([128, 4, 128]), ALU.mult,
                     reads=[pb(b1), "cst"], writes=["attb"])
                for gb_, goff in ((6, 3584), (7, 4608)):
                    for f in range(4):
                        for k in range(8):
                            S.mm(bank(gb_)[:, f * 128:(f + 1) * 128], w_in_sb[:, k, goff + f * 128:goff + (f + 1) * 128], aT[:, k, :],
                                 k == 0, k == 7, reads=["aT", W_IN[k]], writes=[pb(gb_)], sig=(k == 7 and f == 3))
                b2 = gp()
                for h in range(4):
                    o = bank(b2)[:, h * 128:(h + 1) * 128]
                    cross = samp or i > 0
                    S.mm(o, attb[:, h, :], vr_bf[:, h * 128:(h + 1) * 128], True, not cross,
                         reads=["attb", "vr_bf"], writes=[pb(b2)], sig=(not cross and h == 3))
                    if cross:
                        for bq in range(nsq):
                            lhs = qm[:, bq, h, :] if samp else qtT[:, h, :]
                            S.mm(o, lhs, Sbf[:, bq, h, :], False, bq == nsq - 1,
                                 reads=["qm" if samp else "qtT", "Sbf"], writes=[pb(b2)], sig=(bq == nsq - 1 and h == 3))
                S.act(tC[:], bank(b2), AF.Square, reads=[pb(b2)], writes=["tC0", "tC1"])
                S.red("dve", st[:, 56:60], v4(tC[:]), reads=["tC0", "tC1"], writes=["ss4r"])
                rms_rstd(st[:, 56:60], 4, 128, "ss4r", "r4r", st[:, 60:64], st[:, 4:8])
                S.tt("dve", v4(tA[:]), v4(bank(b2)), st[:, 4:8].unsqueeze(2).to_broadcast([128, 4, 128]), ALU.mult,
                     reads=[pb(b2), "r4r"], writes=["tA"])
                S.tt("pool", v4(tB[:]), v4(tA[:]), gs[:, V_GRT:V_GRT + 128].unsqueeze(1).to_broadcast([128, 4, 128]), ALU.mult,
                     reads=["tA", "gs"], writes=["tB"])
                S.tt("pool", or_bf[:], tB[:], sg[:], ALU.mult, reads=["tB", "sg"], writes=["or_bf"])
                last_tile = samp or i == 15
                need_state = samp or True
                for bq in range(nsq):
                    b3 = gp()
                    for h in range(4):
                        lhs = km[:, bq, h, :] if samp else kt_bf[:, h, :]
                        S.mm(bank(b3)[:, h * 128:(h + 1) * 128], lhs, vr_bf[:, h * 128:(h + 1) * 128], True, True,
                             reads=[("km", bq) if samp else "kt_bf", "vr_bf"], writes=[pb(b3)], sig=(h == 3))
                    gb = gam.unsqueeze(2).to_broadcast([128, 4, 128])
                    if first and not samp:
                        S.tt("dve", Sst[:, bq], v4(bank(b3)), gb, ALU.mult, reads=[pb(b3), "cst"], writes=["Sst"])
                    else:
                        S.tt("dve", Sst[:, bq], v4(bank(b3)), Sst[:, bq], ALU.add, reads=[pb(b3), "Sst"], writes=["Sst"])
                        S.tt("pool", Sst[:, bq], Sst[:, bq], gb, ALU.mult, reads=["Sst", "cst"], writes=["Sst"])
                    if not samp:
                        S.cp("pool", Sbf[:, bq], Sst[:, bq], reads=["Sst"], writes=["Sbf"])
                if samp:
                    if "sret" not in _SKIP:
                        S.dma("sp", rso.rearrange("b h d e -> d b h e"), Sst, "rso", reads=["Sst"])
                elif i == 15:
                    S.dma("sp", rp[s].rearrange("h d e -> d h e"), Sst[:, 0], "rpo", reads=["Sst"])

                b = gp()
                for h in range(4):
                    S.tr(bankb(b)[:, h * 128:(h + 1) * 128], od_bf[:, h * 128:(h + 1) * 128], idb[:],
                         reads=["od_bf", "idb"], writes=[pb(b)], sig=False)
                for h in range(4):
                    S.tr(bankb(b)[:, 512 + h * 128:512 + (h + 1) * 128], or_bf[:, h * 128:(h + 1) * 128], idb[:],
                         reads=["or_bf", "idb"], writes=[pb(b)], sig=(h == 3))
                S.cp("act", oT[:].rearrange("p k t -> p (k t)"), bankb(b)[:, 0:1024], reads=[pb(b)], writes=["oT"])
                if nxt is not None:
                    p1_stats(*nxt)
                for r in range(2):
                    gd_, gr_ = (6, 7) if r == 0 else (2, 3)
                    for f in range(4 if r == 1 else 0):
                        fc = r * 4 + f
                        for k in range(8):
                            S.mm(bank(2)[:, f * 128:(f + 1) * 128], w_in_sb[:, k, 3584 + fc * 128:3584 + (fc + 1) * 128], aT[:, k, :],
                                 k == 0, k == 7, reads=["aT", W_IN[k]], writes=[pb(2)], sig=(k == 7 and f == 3))
                    for f in range(4 if r == 1 else 0):
                        fc = r * 4 + f
                        for k in range(8):
                            S.mm(bank(3)[:, f * 128:(f + 1) * 128], w_in_sb[:, k, 4608 + fc * 128:4608 + (fc + 1) * 128], aT[:, k, :],
                                 k == 0, k == 7, reads=["aT", W_IN[k]], writes=[pb(3)], sig=(k == 7 and f == 3))
                    for f in range(4):
                        fc = r * 4 + f
                        for k in range(4):
                            S.mm(bank(4)[:, f * 128:(f + 1) * 128], w_bd_sb[:, k, fc * 128:(fc + 1) * 128], oT[:, k, :],
                                 k == 0, k == 3, reads=["oT", "w_bd"], writes=[pb(4)], sig=(k == 3 and f == 3))
                    for f in range(4):
                        fc = r * 4 + f
                        for k in range(4):
                            S.mm(bank(5)[:, f * 128:(f + 1) * 128], w_br_sb[:, k, fc * 128:(fc + 1) * 128], oT[:, 4 + k, :],
                                 k == 0, k == 3, reads=["oT", "w_br"], writes=[pb(5)], sig=(k == 3 and f == 3))
                    S.act(sgd[:], bank(gd_), AF.Sigmoid, reads=[pb(gd_)], writes=["krow"])
                    S.act(sgr[:], bank(gr_), AF.Sigmoid, reads=[pb(gr_)], writes=["vrow"])
                    S.tt("dve", tA[:], bank(4), sgd[:], ALU.mult, reads=[pb(4), "krow"], writes=["tA"])
                    S.tt("dve", tB[:], bank(5), sgr[:], ALU.mult, reads=[pb(5), "vrow"], writes=["tB"])
                    S.tt("pool", mixT[:, r * 4:(r + 1) * 4, :].rearrange("p k t -> p (k t)"), tA[:], tB[:], ALU.add,
                         reads=["tA", "tB"], writes=[("mixT", r)])
                if nxt is not None:
                    p1_front_a(*nxt)
                for n in range(2):
                    b = gp()
                    for fc in range(8):
                        S.mm(bank(b), mixT[:, fc, :], w_o_sb[:, fc, n * 512:(n + 1) * 512], fc == 0, fc == 7,
                             reads=[("mixT", fc // 4), "w_o"], writes=[pb(b)])
                    S.tt("dve", xt[sl][:, n * 512:(n + 1) * 512], bank(b), xt[sl][:, n * 512:(n + 1) * 512], ALU.add,
                         reads=[pb(b), X], writes=[X])
                S.dma("sp", h_out, xt[sl][:], "h1o", reads=[X])

            tl = ([("s", 0, 0)] if do_sample else []) + [("p", s_, i_) for s_ in range(2) for i_ in range(ntp)]
            tl = [t + (k,) for k, t in enumerate(tl)]
            if not do_sample:
                S.memset("pool", VC[:, :, :, 128:129], 1.0, writes=["VCones"])
            if tl:
                p1_load(*tl[0])
            for k, t in enumerate(tl):
                p1_tile(*t, tl[k + 1] if k + 1 < len(tl) else None)
                if t[0] == "s":
                    S.barrier()
                    S.memset("pool", VC[:, :, :, 128:129], 1.0, writes=["VCones"])

        S.barrier()
        if do_p2:
            with ExitStack() as e2:
                w_g_sb = sbt(e2, "w_g_sb", [128, 8, DFF], BF16)
                w_u_sb = sbt(e2, "w_u_sb", [128, 8, DFF], BF16)
                w_d_sb = sbt(e2, "w_d_sb", [128, NCC, D], BF16)
                w_pe_sb = sbt(e2, "w_pe_sb", [128, 2, D], BF16)
                w_pg_sb = sbt(e2, "w_pg_sb", [128, 8, D], BF16)
                wgv = w_g.rearrange("(k p) n -> p k n", p=128)
                wuv = w_u.rearrange("(k p) n -> p k n", p=128)
                for k in range(8):
                    S.dma("pool", w_g_sb[:, k, :], wgv[:, k, :], ("w_g", k), writes=[("w_g", k)])
                    S.dma("pool", w_u_sb[:, k, :], wuv[:, k, :], ("w_u", k), writes=[("w_u", k)])
                wdv = w_d.rearrange("(k p) n -> p k n", p=128)
                for c0 in range(0, NCC, 8):
                    c1 = min(NCC, c0 + 8)
                    S.dma("pool", w_d_sb[:, c0:c1, :], wdv[:, c0:c1, :], ("w_d", c0), writes=[("w_d", c0)])
                WD = [("w_d", (c // 8) * 8) for c in range(NCC)]
                S.dma("pool", w_pe_sb[:], w_pe.rearrange("(k p) n -> p k n", p=128), "w_pe", writes=["w_pe"])
                S.dma("pool", w_pg_sb[:], w_pg.rearrange("(k p) n -> p k n", p=128), "w_pg", writes=["w_pg"])

                ht = [sbt(e2, "ht%d" % i, [128, D], F32) for i in range(2)]
                p_bf = [sbt(e2, "p_bf%d" % i, [128, 256], BF16) for i in range(2)]
                f_bf = sbt(e2, "f_bf", [128, D], BF16)
                fT = sbt(e2, "fT", [128, 8, 128], BF16)
                f3T = sbt(e2, "f3T", [128, 8, 128], BF16)
                pT = sbt(e2, "pT", [128, 2, 128], BF16)
                ft8 = sbt(e2, "ft8", [128, 8, 8], BF16)
                st2 = sbt(e2, "st2", [128, 16], F32)
                Gt = sbt(e2, "Gt", [128, NCC * 4 * 34], F32)
                Gp = Gt[:, 0:NCC * 130].rearrange("p (c l) -> p c l", c=NCC)
                Gs = Gt[:].rearrange("p (c b l) -> p c b l", c=NCC, b=4)
                C4 = sbt(e2, "C4", [128, 4, 512], F32)
                cA, cB, cC, glu = C4[:, 0, :], C4[:, 1, :], C4[:, 2, :], C4[:, 3, :]
                sg3 = C4[:, 0:2, :].rearrange("p a b -> p (a b)")
                hhT = sbt(e2, "hhT", [128, NCC, 128], BF16)
                yt_t = sbt(e2, "yt", [128, D], F32)
                yt = yt_t[:]
                f3_bf = sbt(e2, "f3_bf", [128, D], BF16)
                sg3h = sbt(e2, "sg3h", [128, 512], F32)
                cnv = [sbt(e2, "cnv%d" % i, [8, 512], F32) for i in range(1)]

                def p2_load(kind, s, i, tix):
                    sl = tix % 2
                    if kind == "s":
                        h_in, p_ap = h1s, psm
                    else:
                        rows = slice(i * 128, (i + 1) * 128)
                        h_in, p_ap = h1p[s, rows, :], pp[s, rows, :]
                    S.dma("sp", ht[sl][:], h_in, ("ht", sl), writes=[("ht", sl)])
                    S.dma("pool", p_bf[sl][:], p_ap, ("p_bf", sl), writes=[("p_bf", sl)])

                def p2_front_a(kind, s, i, tix):
                    sl = tix % 2
                    H = ("ht", sl)
                    S.act(f_bf[:], ht[sl][:], AF.Square, accum_out=st2[:, 0:1], reads=[H], writes=["f_bf", "q0"])
                    rms_rstd(st2[:, 0:1], 1, D, "q0", "q2", st2[:, 1:2], st2[:, 2:3])
                    S.op("dve", lambda e: e.tensor_scalar_mul(f_bf[:], ht[sl][:], st2[:, 2:3]),
                         reads=[H, "q2"], writes=["f_bf"])

                def p2_front_b(kind, s, i, tix):
                    b = gp()
                    for k in range(8):
                        S.tr(bankb(b)[:, k * 128:(k + 1) * 128], f_bf[:, k * 128:(k + 1) * 128], idb[:],
                             reads=["f_bf", "idb"], writes=[pb(b)], sig=(k == 7))
                    S.tt("dve", fT[:], bankb(b)[:, 0:1024].rearrange("p (k t) -> p k t", k=8),
                         gfm[:, 1, :].unsqueeze(2).to_broadcast([128, 8, 128]), ALU.mult, reads=[pb(b), "gfm"], writes=["fT"])

                def p2_mid(kind, s, i, tix, filler, after_filler):
                    sl = tix % 2
                    samp = kind == "s"
                    H = ("ht", sl)
                    fstate = [filler, False]

                    def step():
                        if fstate[0] is not None:
                            try:
                                next(fstate[0])
                            except StopIteration:
                                fstate[0] = None
                        if fstate[0] is None and not fstate[1]:
                            fstate[1] = True
                            after_filler()

                    if filler is None:
                        step()
                    if samp:
                        S.dma("sp", Gs[:, :, :, 0:2], scv, "scv", writes=["Gc"])
                    elif i == 0:
                        S.memset("pool", Gp[:, :, 0:2], 0.0, writes=["Gc"])
                    gi = 0
                    for c0 in range(0, NCC, 4):
                        n = min(4, NCC - c0)
                        w = n * 128
                        bg, bu = 2 + (gi % 2), 4 + (gi % 2)
                        gi += 1
                        for f in range(n):
                            cc = c0 + f
                            for k in range(8):
                                S.mm(bank(bg)[:, f * 128:(f + 1) * 128], w_g_sb[:, k, cc * 128:(cc + 1) * 128], fT[:, k, :],
                                     k == 0, k == 7, reads=["fT", ("w_g", k)], writes=[pb(bg)], sig=(k == 7 and f == n - 1))
                        for f in range(n):
                            cc = c0 + f
                            for k in range(8):
                                S.mm(bank(bu)[:, f * 128:(f + 1) * 128], w_u_sb[:, k, cc * 128:(cc + 1) * 128], fT[:, k, :],
                                     k == 0, k == 7, reads=["fT", ("w_u", k)], writes=[pb(bu)], sig=(k == 7 and f == n - 1))
                        if samp:
                            gsrc = bank(bg)[:, 0:w].rearrange("p (c b l) -> p c b l", c=n, b=4)
                            gdst = Gs[:, c0:c0 + n, :, 2:34]

                            def tap(j):
                                return Gs[:, c0:c0 + n, :, j:j + 32]

                            def shp(ap):
                                return ap.rearrange("p (c b l) -> p c b l", c=n, b=4)

                            def wb_(j):
                                return cw_sb[:, j, c0:c0 + n].unsqueeze(2).unsqueeze(3).to_broadcast([128, n, 4, 32])
                        else:
                            gsrc = bank(bg)[:, 0:w].rearrange("p (c l) -> p c l", c=n)
                            gdst = Gp[:, c0:c0 + n, 2:130]

                            def tap(j):
                                return Gp[:, c0:c0 + n, j:j + 128]

                            def shp(ap):
                                return ap.rearrange("p (c l) -> p c l", c=n)

                            def wb_(j):
                                return cw_sb[:, j, c0:c0 + n].unsqueeze(2).to_broadcast([128, n, 128])
                        GR = ("G", c0)
                        S.cp("act", gdst, gsrc, reads=[pb(bg), "Gc"], writes=[GR])
                        ab = gi % 2
                        CA, GL = ("cacc", ab), ("glu", ab)
                        acc_t = C4[:, ab, :]
                        glu_t = C4[:, 2 + ab, :]
                        accs, taps = [], []
                        for f in range(n):
                            cc = c0 + f
                            if samp:
                                acc = acc_t[:, f * 128:(f + 1) * 128].rearrange("p (b l) -> p b l", b=4)
                                gps = bank(bg)[:, f * 128:(f + 1) * 128].rearrange("p (b l) -> p b l", b=4)
                                tp = (Gs[:, cc, :, 0:32], Gs[:, cc, :, 1:33])
                            else:
                                acc = acc_t[:, f * 128:(f + 1) * 128]
                                gps = bank(bg)[:, f * 128:(f + 1) * 128]
                                tp = (Gp[:, cc, 0:128], Gp[:, cc, 1:129])
                            accs.append(acc)
                            taps.append(tp)
                            S.act(acc, gps, AF.Identity, scale=cw_sb[:, 2, cc:cc + 1], bias=cw_sb[:, 3, cc:cc + 1],
                                  reads=[pb(bg), "cw"], writes=[(CA, f)])
                        for j in (1, 0):
                            for f in range(n):
                                cc = c0 + f
                                eng = "dve"
                                S.op(eng, lambda e, acc=accs[f], t=taps[f][j], cc=cc, j=j: e.scalar_tensor_tensor(
                                    acc, t, cw_sb[:, j, cc:cc + 1], acc, ALU.mult, ALU.add),
                                    reads=[GR, "Gc", "cw", (CA, f)], writes=[(CA, f)])
                        CAs = [(CA, f) for f in range(n)]
                        S.act(glu_t[:, 0:w], acc_t[:, 0:w], AF.Gelu_apprx_tanh, reads=CAs, writes=[GL] + [(CA, f) for f in range(4)])
                        S.tt("dve", hhT[:, c0:c0 + n, :].rearrange("p c t -> p (c t)"), bank(bu)[:, 0:w], glu_t[:, 0:w], ALU.mult,
                             reads=[pb(bu), GL], writes=[("hhT", c0)])
                        step()
                    while fstate[0] is not None or not fstate[1]:
                        step()
                    GALL = [("G", c0) for c0 in range(0, NCC, 4)]
                    if not samp and i < 15:
                        S.cp("pool", Gp[:, :, 0:2], Gp[:, :, 128:130], reads=GALL, writes=["Gc"] + GALL)
                    if samp or i == 15:
                        if samp:
                            S.cp("dve", ft8[:].rearrange("p k (b j) -> p k b j", b=4),
                                 fT[:].rearrange("p k (b l) -> p k b l", b=4)[:, :, :, 30:32], reads=["fT"], writes=["ft8"])
                            M = 8
                        else:
                            S.cp("dve", ft8[:, :, 0:2], fT[:, :, 126:128], reads=["fT"], writes=["ft8"])
                            M = 2
                        for c0 in range(0, DFF, 512):
                            w = min(512, DFF - c0)
                            b = gp()
                            for k in range(8):
                                S.mm(bank(b)[0:M, 0:w], ft8[:, k, 0:M], w_g_sb[:, k, c0:c0 + w], k == 0, k == 7,
                                     reads=["ft8", ("w_g", k)], writes=[pb(b)])
                            ci = 0
                            S.cp("act", cnv[ci][0:M, 0:w], bank(b)[0:M, 0:w], reads=[pb(b)], writes=[("cnv", ci)])
                            dst = cso.rearrange("b j c -> (b j) c") if samp else cpo[s]
                            S.dma("sp", dst[:, c0:c0 + w], cnv[ci][0:M, 0:w], ("cnvo", ci), reads=[("cnv", ci)])
                    yield
                    for n_ in range(2):
                        b = gp()
                        for cc in range(NCC):
                            S.mm(bank(b), hhT[:, cc, :], w_d_sb[:, cc, n_ * 512:(n_ + 1) * 512], cc == 0, cc == NCC - 1,
                                 reads=[("hhT", (cc // 4) * 4), WD[cc]], writes=[pb(b)])
                        S.tt("dve", ht[sl][:, n_ * 512:(n_ + 1) * 512], bank(b), ht[sl][:, n_ * 512:(n_ + 1) * 512], ALU.add,
                             reads=[pb(b), H], writes=[H])

                def p2_back(kind, s, i, tix):
                    sl = tix % 2
                    samp = kind == "s"
                    y_out = ys if samp else yp[s, i * 128:(i + 1) * 128, :]
                    H, Pp = ("ht", sl), ("p_bf", sl)
                    S.act(f3_bf[:], ht[sl][:], AF.Square, accum_out=st2[:, 4:5], reads=[H], writes=["f3_bf", "q4"])
                    rms_rstd(st2[:, 4:5], 1, D, "q4", "q6", st2[:, 5:6], st2[:, 6:7])
                    S.op("dve", lambda e: e.tensor_scalar_mul(f3_bf[:], ht[sl][:], st2[:, 6:7]),
                         reads=[H, "q6"], writes=["f3_bf"])
                    yield
                    b = gp()
                    for k in range(8):
                        S.tr(bankb(b)[:, k * 128:(k + 1) * 128], f3_bf[:, k * 128:(k + 1) * 128], idb[:],
                             reads=["f3_bf", "idb"], writes=[pb(b)], sig=(k == 7))
                    S.tt("dve", f3T[:], bankb(b)[:, 0:1024].rearrange("p (k t) -> p k t", k=8),
                         gfm[:, 2, :].unsqueeze(2).to_broadcast([128, 8, 128]), ALU.mult, reads=[pb(b), "gfm"], writes=["f3T"])
                    b = gp()
                    for k in range(2):
                        S.tr(bankb(b)[:, k * 128:(k + 1) * 128], p_bf[sl][:, k * 128:(k + 1) * 128], idb[:],
                             reads=[Pp, "idb"], writes=[pb(b)], sig=(k == 1))
                    S.cp("act", pT[:].rearrange("p k t -> p (k t)"), bankb(b)[:, 0:256], reads=[pb(b)], writes=["pT"])
                    yield
                    for n_ in range(2):
                        b = gp()
                        for k in range(8):
                            S.mm(bank(b), f3T[:, k, :], w_pg_sb[:, k, n_ * 512:(n_ + 1) * 512], k == 0, k == 7,
                                 reads=["f3T", "w_pg"], writes=[pb(b)])
                        S.act(sg3h[:], bank(b), AF.Sigmoid, reads=[pb(b)], writes=["sg3h"])
                        b2 = 6 + n_
                        for k in range(2):
                            S.mm(bank(b2), pT[:, k, :], w_pe_sb[:, k, n_ * 512:(n_ + 1) * 512], k == 0, k == 1,
                                 reads=["pT", "w_pe"], writes=[pb(b2)])
                        S.tt("dve", yt[:, n_ * 512:(n_ + 1) * 512], bank(b2), sg3h[:], ALU.mult,
                             reads=[pb(b2), "sg3h"], writes=[("yt", n_)])
                        S.tt("pool", yt[:, n_ * 512:(n_ + 1) * 512], yt[:, n_ * 512:(n_ + 1) * 512], ht[sl][:, n_ * 512:(n_ + 1) * 512],
                             ALU.add, reads=[("yt", n_), H], writes=[("yt", n_)])
                        yield
                    S.dma("sp", y_out, yt[:], "yo", reads=[("yt", 0), ("yt", 1)])

                tl = ([("s", 0, 0)] if do_sample else []) + [("p", s_, i_) for s_ in range(2) for i_ in range(ntp)]
                tl = [t + (k,) for k, t in enumerate(tl)]
                prev = None
                if tl:
                    p2_load(*tl[0])
                    p2_front_a(*tl[0])
                    p2_front_b(*tl[0])
                for j, t in enumerate(tl):
                    if j + 1 < len(tl):
                        after = (lambda j=j: p2_load(*tl[j + 1]))
                    else:
                        after = (lambda: None)
                    gmid = p2_mid(*t, prev, after)
                    next(gmid)
                    if j + 1 < len(tl):
                        p2_front_a(*tl[j + 1])
                    for _ in gmid:
                        pass
                    if j + 1 < len(tl):
                        p2_front_b(*tl[j + 1])
                    prev = p2_back(*t)
                if prev is not None:
                    for _ in prev:
                        pass
                S.emit()
        else:
            S.emit()
    return nc


def _t5_bucket(rel):
    rel = np.asarray(rel, np.int64)
    nb, me = 16, 8
    ret = np.where(rel > 0, nb, 0)
    n = np.abs(rel)
    nf = np.maximum(n, 1).astype(np.float32)
    large = me + (np.log(nf / np.float32(me)).astype(np.float32) / np.float32(math.log(128 / me))
                  * np.float32(nb - me)).astype(np.int32)
    large = np.minimum(large, nb - 1)
    return ret + np.where(n < me, n, large)


def _consts():
    c = np.zeros((128, NCST), np.float32)
    c2 = np.zeros((128, NC2), np.float32)
    c2[:, C2_ID:C2_ID + 128] = np.eye(128, dtype=np.float32)
    key = np.arange(128)[:, None]
    q = np.arange(128)[None, :]
    c2[:, C2_M0:C2_M0 + 128] = np.where((key // 64) > (q // 64), NEG, 0.0)
    c2[:, C2_MN:C2_MN + 128] = np.where((key // 32) != (q // 32), NEG, 0.0)
    c[:, C_RMP:C_RMP + 128] = (key <= q).astype(np.float32)
    c[:, C_RMS:C_RMS + 128] = ((key // 32 == q // 32) & (key <= q)).astype(np.float32)
    for b in range(4):
        c2[:, C2_SMQ + b * 128:C2_SMQ + (b + 1) * 128] = (q // 32 == b).astype(np.float32)
        c[:, C_KM + b] = (np.arange(128) // 32 == b).astype(np.float32)
    gam = 1.0 - np.power(2.0, -5.0 - np.arange(4, dtype=np.float64))
    c[:, C_GAM:C_GAM + 4] = (gam ** 128)[None, :]
    c[:, C_GAM + 4:C_GAM + 8] = (gam ** 32)[None, :]
    for kind, n in ((0, np.arange(128)), (1, np.arange(128) % 32)):
        for h in range(4):
            c[:, C_DQK + kind * 8 + h] = gam[h] ** (n + 1.0)
            c[:, C_DQK + kind * 8 + 4 + h] = gam[h] ** (-(n + 1.0)) * (128 ** -0.5)
    return c, c2


def _rot_tables():
    half = 64
    inv = np.power(np.float32(10000.0), -np.arange(half, dtype=np.float32) / np.float32(half)).astype(np.float32)
    out = np.zeros((17, 128, 2, 128), np.float32)
    for t in range(17):
        if t < 16:
            pos = t * 128 + np.arange(128)
        else:
            pos = 4096 + (np.arange(128) % 32)
        ang = (pos[:, None].astype(np.float32) * inv[None, :]).astype(np.float32)
        cs, sn = np.cos(ang.astype(np.float64)), np.sin(ang.astype(np.float64))
        out[t, :, 0, :] = np.concatenate([cs, cs], 1)
        out[t, :, 1, :] = np.concatenate([-sn, sn], 1)
    return out


_NC_CACHE = {}
import os as _os
_SKIP = _os.environ.get("KSKIP", "").split(",")


def kernel(x_prompt, x_sample, p_prompt, p_sample, cache_k, cache_v, state_ret, state_conv,
           rel_bias, g_mix, w_in, g_q, g_k, lam_q1, lam_k1, lam_q2, lam_k2, g_da, g_rt,
           w_bd, w_br, w_o, g_ffn, w_g, w_u, conv_w, conv_b, w_d, g_pe, w_pe, w_pg, _ntp=16, _samp=True, _p2=True):
    f = lambda a: np.ascontiguousarray(np.asarray(a, np.float32))
    x_prompt, x_sample, p_prompt, p_sample = f(x_prompt), f(x_sample), f(p_prompt)[0], f(p_sample)[0]
    cache_k, cache_v, state_ret, state_conv = f(cache_k)[0], f(cache_v)[0], f(state_ret)[0], f(state_conv)[0]
    rel_bias = f(rel_bias)
    rep = lambda v: np.broadcast_to(f(v).reshape(1, -1), (128, f(v).size))
    gfmh = np.ascontiguousarray(np.stack([f(g_mix).reshape(8, 128).T, f(g_ffn).reshape(8, 128).T,
                                          f(g_pe).reshape(8, 128).T], 1))
    gsm = np.ascontiguousarray(np.concatenate(
        [rep(g_q), rep(g_k), rep(lam_q1), rep(lam_k1), rep(lam_q2), rep(lam_k2), rep(g_da), rep(g_rt)], 1))
    cwt = np.concatenate([f(conv_w)[0], f(conv_b)], 0)
    cw = np.ascontiguousarray(cwt.reshape(4, NCC, 128).transpose(2, 0, 1))
    key = np.arange(128)[:, None]
    q = np.arange(128)[None, :]
    bk = np.stack([_t5_bucket(key - q), _t5_bucket(key - 128 - q), _t5_bucket((key % 32) - (q % 32))], 0)
    DRh = np.ascontiguousarray(rel_bias[bk].transpose(1, 0, 3, 2))
    CHh = np.ascontiguousarray(np.broadcast_to(rel_bias[15][None, :], (128, 4)))
    cst, cst2 = _consts()
    rot = _rot_tables()
    shared = dict(w_in=f(w_in)[0], w_bd=f(w_bd)[0], w_br=f(w_br)[0], w_o=f(w_o)[0], w_g=f(w_g)[0], w_u=f(w_u)[0],
                  w_d=f(w_d)[0], w_pe=f(w_pe)[0], w_pg=f(w_pg)[0], gfm=gfmh, gsm=gsm, cw=cw, DR=DRh, CH=CHh,
                  cst=cst, cst2=cst2, rot=rot)
    in_maps = []
    for c in range(8):
        m = dict(shared)
        m["xp"] = x_prompt[2 * c:2 * c + 2]
        m["xs"] = x_sample[4 * c:4 * c + 4].reshape(128, D)
        m["pp"] = p_prompt[2 * c:2 * c + 2]
        m["psm"] = p_sample[4 * c:4 * c + 4].reshape(128, 256)
        m["ck"] = cache_k[4 * c:4 * c + 4].reshape(4, 4096, 512)
        m["cv"] = cache_v[4 * c:4 * c + 4].reshape(4, 4096, 512)
        m["sret"] = state_ret[4 * c:4 * c + 4]
        sc = state_conv[4 * c:4 * c + 4]
        m["scv"] = np.ascontiguousarray(sc.reshape(4, 2, NCC, 128).transpose(3, 2, 0, 1))
        in_maps.append(m)
    ck_ = (_ntp, _samp, _p2)
    if ck_ not in _NC_CACHE:
        _NC_CACHE[ck_] = build(ntp=_ntp, do_sample=_samp, do_p2=_p2)
    ncr = int(_os.environ.get("KNCORES", "8"))
    res = run_bass_kernel_spmd(_NC_CACHE[ck_], in_maps[:ncr], core_ids=list(range(ncr)))
    R = list(res.results) + [res.results[0]] * (8 - ncr)
    cat = lambda k: np.concatenate([r[k] for r in R], 0)
    y_p = cat("yp")
    y_s = cat("ys").reshape(32, 32, D)
    k_p = cat("kp").reshape(1, 16, 2048, 4, 128)
    v_p = cat("vp").reshape(1, 16, 2048, 4, 128)
    r_p = cat("rp").reshape(1, 16, 4, 128, 128)
    c_p = cat("cpo").reshape(1, 16, 2, DFF)
    k_s = cat("kso").reshape(1, 32, 32, 4, 128)
    v_s = cat("vso").reshape(1, 32, 32, 4, 128)
    r_s = cat("rso").reshape(1, 32, 4, 128, 128)
    c_s = cat("cso").reshape(1, 32, 2, DFF)
    return (y_p, y_s, k_p, v_p, r_p, c_p, k_s, v_s, r_s, c_s)
```

```python
import math
from contextlib import ExitStack

import numpy as np
import concourse.bass as bass
import concourse.mybir as mybir
from concourse.bass_utils import run_bass_kernel_spmd

F32, BF16 = mybir.dt.float32, mybir.dt.bfloat16
AF = mybir.ActivationFunctionType
ALU = mybir.AluOpType
AX = mybir.AxisListType
ENGS = ("pe", "act", "dve", "pool", "sp")

D = 1024
DFF = 2816
NCC = 22
DIN = 5632
EPS = 1e-6
NEG = -30000.0


class Sched:
    EPOCH = 30000

    def __init__(self, nc, es):
        self.nc, self.es = nc, es
        self.ops = {e: [] for e in ENGS}
        self.esem = {e: None for e in ENGS}
        self.ecnt = {e: 0 for e in ENGS}
        self.last = {e: None for e in ENGS}
        self.dsem = {}
        self.res = {}
        self.seen = {e: {} for e in ENGS}
        self.nsem = 0
        self.pend = {e: [] for e in ENGS}

    def _newsem(self, name):
        self.nsem += 1
        return self.es.enter_context(self.nc.semaphore("s%d_%s" % (self.nsem, name)))

    def barrier(self):
        ticks = [t for t in self.last.values() if t is not None]
        ticks += [(d[0], d[1], None) for d in self.dsem.values() if d[1] > 0]
        for e in ENGS:
            self.pend[e] = list(ticks)

    def op(self, eng, fn, reads=(), writes=(), sig=True, dma_key=None):
        waits = {}

        def add(t, raw, force=False):
            sem, val, teng = t
            if not force and teng == eng and eng == "pe":
                return
            k = id(sem)
            if self.seen[eng].get(k, 0) >= val:
                return
            if k not in waits or waits[k][1] < val:
                waits[k] = (sem, val)

        for t in self.pend[eng]:
            add(t, True, force=(t[2] != eng))
        self.pend[eng] = []
        for r in reads:
            st = self.res.get(r)
            if st and st["w"]:
                add(st["w"], True)
        for w in writes:
            st = self.res.get(w)
            if st:
                if st["w"]:
                    add(st["w"], False)
                for t in st["r"]:
                    add(t, False)
        for k, (sem, val) in waits.items():
            self.seen[eng][k] = val
        inc = None
        if dma_key is not None:
            if dma_key not in self.dsem:
                self.dsem[dma_key] = [self._newsem("d"), 0]
            d = self.dsem[dma_key]
            d[1] += 16
            tick = (d[0], d[1], None)
            inc = (d[0], 16)
        else:
            if self.esem[eng] is None:
                self.esem[eng] = self._newsem(eng)
                self.ecnt[eng] = 0
            if sig:
                self.ecnt[eng] += 1
                tick = (self.esem[eng], self.ecnt[eng], eng)
                inc = (self.esem[eng], 1)
                if self.ecnt[eng] >= self.EPOCH:
                    self.esem[eng] = None
            else:
                tick = (self.esem[eng], self.ecnt[eng] + 1, eng)
            self.last[eng] = tick
        for r in reads:
            self.res.setdefault(r, {"w": None, "r": []})["r"].append(tick)
        for w in writes:
            self.res[w] = {"w": tick, "r": []}
        self.ops[eng].append((list(waits.values()), fn, inc))

    def emit(self):
        nc = self.nc
        final = [(d[0], d[1]) for d in self.dsem.values()]
        ops = self.ops
        with nc.Block() as block:
            def run(name, e, tail=False):
                for waits, fn, inc in ops[name]:
                    for sem, val in waits:
                        e.wait_ge(sem, val)
                    ins = fn(e)
                    if inc is not None:
                        ins.then_inc(inc[0], inc[1])
                if tail:
                    for sem, val in final:
                        e.wait_ge(sem, val)

            @block.tensor
            def _(e):
                run("pe", e)

            @block.scalar
            def _(e):
                run("act", e)

            @block.vector
            def _(e):
                run("dve", e)

            @block.gpsimd
            def _(e):
                run("pool", e)

            @block.sync
            def _(e):
                run("sp", e, tail=True)

    def dma(self, q, out, in_, key, reads=(), writes=()):
        self.op(q, lambda e: e.dma_start(out=out, in_=in_), reads, writes, dma_key=key)

    def mm(self, out, lhsT, rhs, start, stop, reads=(), writes=(), sig=None, **kw):
        if sig is None:
            sig = stop
        self.op("pe", lambda e: e.matmul(out, lhsT, rhs, start=start, stop=stop, **kw),
                reads, writes, sig=sig)

    def tr(self, out, in_, ident, reads=(), writes=(), sig=True):
        self.op("pe", lambda e: e.transpose(out, in_, ident), reads, writes, sig=sig)

    def act(self, out, in_, func, reads=(), writes=(), **kw):
        self.op("act", lambda e: e.activation(out, in_, func, **kw), reads, writes)

    def tt(self, eng, out, in0, in1, op, reads=(), writes=()):
        self.op(eng, lambda e: e.tensor_tensor(out, in0, in1, op), reads, writes)

    def ts(self, eng, out, in0, s1, s2, op0, op1, reads=(), writes=()):
        self.op(eng, lambda e: e.tensor_scalar(out, in0, s1, s2, op0, op1), reads, writes)

    def cp(self, eng, out, in_, reads=(), writes=()):
        if eng == "act":
            self.op(eng, lambda e: e.copy(out, in_), reads, writes)
        else:
            self.op(eng, lambda e: e.tensor_copy(out, in_), reads, writes)

    def red(self, eng, out, in_, reads=(), writes=()):
        self.op(eng, lambda e: e.tensor_reduce(out, in_, AX.X, ALU.add), reads, writes)

    def recip(self, out, in_, reads=(), writes=()):
        self.op("dve", lambda e: e.reciprocal(out, in_), reads, writes)

    def memset(self, eng, ap, val, reads=(), writes=()):
        self.op(eng, lambda e: e.memset(ap, val), reads, writes)


C_RMP, C_RMS, C_KM, C_GAM, C_DQK, NCST = 0, 128, 256, 260, 268, 284
C2_ID, C2_M0, C2_MN, C2_SMQ, NC2 = 0, 128, 256, 384, 896
V_GQ, V_GK, V_L, V_GDA, V_GRT, NV = 0, 64, 128, 384, 512, 640


def build(ntp=16, do_sample=True, do_p2=True):
    nc = bass.Bass("TRN2", target_bir_lowering=False)

    def din(name, shape):
        return nc.dram_tensor(name, list(shape), F32, kind="ExternalInput").ap()

    def dout(name, shape):
        return nc.dram_tensor(name, list(shape), F32, kind="ExternalOutput").ap()

    xp = din("xp", [2, 2048, D]); xs = din("xs", [128, D])
    pp = din("pp", [2, 2048, 256]); psm = din("psm", [128, 256])
    ck = din("ck", [4, 4096, 512]); cv = din("cv", [4, 4096, 512])
    sret = din("sret", [4, 4, 128, 128]); scv = din("scv", [128, NCC, 4, 2])
    w_in = din("w_in", [D, DIN]); w_bd = din("w_bd", [512, D]); w_br = din("w_br", [512, D])
    w_o = din("w_o", [D, D]); w_g = din("w_g", [D, DFF]); w_u = din("w_u", [D, DFF])
    w_d = din("w_d", [DFF, D]); w_pe = din("w_pe", [256, D]); w_pg = din("w_pg", [D, D])
    gfmd = din("gfm", [128, 3, 8]); gsm = din("gsm", [128, NV]); cw = din("cw", [128, 4, NCC])
    DR = din("DR", [128, 3, 4, 128]); CH = din("CH", [128, 4]); cst = din("cst", [128, NCST])
    cst2 = din("cst2", [128, NC2])
    rot = din("rot", [17, 128, 2, 128])
    yp = dout("yp", [2, 2048, D]); ys = dout("ys", [128, D])
    kp = dout("kp", [2, 2048, 512]); vp = dout("vp", [2, 2048, 512])
    rp = dout("rp", [2, 4, 128, 128]); cpo = dout("cpo", [2, 2, DFF])
    kso = dout("kso", [128, 512]); vso = dout("vso", [128, 512])
    rso = dout("rso", [4, 4, 128, 128]); cso = dout("cso", [4, 2, DFF])
    h1p = nc.dram_tensor("h1p", [2, 2048, D], F32, kind="Internal").ap()
    h1s = nc.dram_tensor("h1s", [128, D], F32, kind="Internal").ap()

    with ExitStack() as es:
        S = Sched(nc, es)
        PS = es.enter_context(nc.psum_tensor("PS", [128, 4096], F32))

        def bank(k):
            return PS[:, k * 512:(k + 1) * 512]

        def bankb(k):
            return PS[:, k * 512:(k + 1) * 512].bitcast(BF16)

        def pb(k):
            return ("ps", k)

        gpc = [0]

        def gp():
            gpc[0] ^= 1
            return gpc[0]

        def sbt(st, name, shape, dt):
            return st.enter_context(nc.sbuf_tensor(name, list(shape), dt))

        cst_sb = sbt(es, "cst_sb", [128, NCST], F32)
        idb = sbt(es, "idb", [128, 128], BF16)
        gfm = sbt(es, "gfm_sb", [128, 3, 8], F32)
        gs = sbt(es, "gs", [128, NV], F32)
        cw_sb = sbt(es, "cw_sb", [128, 4, NCC], F32)
        lam = sbt(es, "lam", [128, 8], F32)
        S.dma("sp", cst_sb[:], cst, "c0", writes=["cst"])
        S.dma("pool", idb[:], cst2[:, C2_ID:C2_ID + 128], "c1", writes=["idb"])
        S.dma("sp", gfm[:], gfmd, "c2", writes=["gfm"])
        S.dma("sp", gs[:], gsm, "c3", writes=["gs"])
        S.dma("sp", cw_sb[:], cw, "c4", writes=["cw"])

        def rms_rstd(ss_ap, n, dim, tagr, tagw, tmp_ap, out_ap):
            S.ts("dve", tmp_ap, ss_ap, 1.0 / dim, EPS, ALU.mult, ALU.add, reads=[tagr], writes=[tagw + "_a"])
            S.act(tmp_ap, tmp_ap, AF.Ln, reads=[tagw + "_a"], writes=[tagw + "_b"])
            S.act(out_ap, tmp_ap, AF.Exp, reads=[tagw + "_b"], writes=[tagw], scale=-0.5)

        with ExitStack() as e1:
            w_in_sb = sbt(e1, "w_in_sb", [128, 8, DIN], BF16)
            w_bd_sb = sbt(e1, "w_bd_sb", [128, 4, D], BF16)
            w_br_sb = sbt(e1, "w_br_sb", [128, 4, D], BF16)
            w_o_sb = sbt(e1, "w_o_sb", [128, 8, D], BF16)
            def v4(ap):
                return ap.rearrange("p (a b) -> p a b", a=4)

            w_in_v = w_in.rearrange("(k p) n -> p k n", p=128)
            for k in range(8):
                S.dma("pool", w_in_sb[:, k, :], w_in_v[:, k, :], ("w_in", k), writes=[("w_in", k)])
            S.dma("pool", w_bd_sb[:], w_bd.rearrange("(k p) n -> p k n", p=128), "w_bd", writes=["w_bd"])
            S.dma("pool", w_br_sb[:], w_br.rearrange("(k p) n -> p k n", p=128), "w_br", writes=["w_br"])
            S.dma("pool", w_o_sb[:], w_o.rearrange("(k p) n -> p k n", p=128), "w_o", writes=["w_o"])
            W_IN = [("w_in", k) for k in range(8)]

            T3 = sbt(e1, "T3", [128, 3, 512], F32)
            tA, tB, tC = T3[:, 0, :], T3[:, 1, :], T3[:, 2, :]
            DRs = T3[:].rearrange("p k (h q) -> p k h q", h=4)
            vrow_t = sbt(e1, "vrow", [128, 512], F32)
            od = vrow_t
            sg = sbt(e1, "sg", [128, 512], F32)
            CHs = sbt(e1, "CHs", [128, 4], F32)
            DH = sbt(e1, "DH", [128, 3, 4, 128], BF16)
            DL = sbt(e1, "DL", [128, 3, 4, 128], BF16)
            smq = sbt(e1, "smq", [128, 4, 128], BF16)
            S.dma("sp", DRs, DR, "c5", writes=["DRs"])
            S.dma("sp", CHs[:], CH, "c6", writes=["CHs"])
            S.dma("sp", sg[:, 0:256], cst2[:, C2_M0:C2_M0 + 256], "c8", writes=["msk"])
            S.dma("pool", smq[:], cst2[:, C2_SMQ:C2_SMQ + 512].rearrange("p (b q) -> p b q", b=4), "c7", writes=["smq"])
            for kd in range(3):
                S.tt("dve", DRs[:, kd], DRs[:, kd], CHs[:].unsqueeze(2).to_broadcast([128, 4, 128]), ALU.subtract,
                     reads=["DRs", "CHs"], writes=["DRs"])
            for kd, off in ((0, 0), (2, 128)):
                S.tt("dve", DRs[:, kd], DRs[:, kd], sg[:, off:off + 128].unsqueeze(1).to_broadcast([128, 4, 128]),
                     ALU.add, reads=["DRs", "msk"], writes=["DRs"])
            S.cp("dve", DH[:], DRs, reads=["DRs"], writes=["DH"])
            for kd in range(3):
                S.tt("dve", v4(od[:]), DRs[:, kd], DH[:, kd], ALU.subtract, reads=["DRs", "DH"], writes=["odtmp"])
                S.cp("dve", DL[:, kd], v4(od[:]), reads=["odtmp"], writes=["DL"])
            junk64 = sbt(e1, "junk64", [128, 2, 64], F32)
            gl = gs[:, V_L:V_L + 256].rearrange("p (a b c) -> p a b c", a=2, b=2)
            S.tt("dve", junk64[:], gl[:, :, 0, :], gl[:, :, 1, :], ALU.mult, reads=["gs"], writes=["junk64"])
            S.red("dve", lam[:, 0:2], junk64[:], reads=["junk64"], writes=["lam0"])
            S.act(lam[:, 4:6], lam[:, 0:2], AF.Exp, reads=["lam0"], writes=["lam1"])
            S.tt("dve", lam[:, 2:3], lam[:, 5:6], lam[:, 4:5], ALU.subtract, reads=["lam1"], writes=["lam2"])
            S.ts("dve", lam[:, 3:4], lam[:, 2:3], 1.0, -0.2, ALU.mult, ALU.add, reads=["lam2"], writes=["lam"])
            NLAM = lam[:, 3:4]

            ARENA = sbt(e1, "ARENA", [128, 9032], F32)
            AB = ARENA[:].bitcast(BF16)
            KT = AB[:, 0:8192].rearrange("p (h t) -> p h t", h=4)
            VC = AB[:, 8192:16512].rearrange("p (j h e) -> p j h e", j=16, h=4)
            Sst_p = ARENA[:, 8256:8768].rearrange("p (b h e) -> p b h e", b=1, h=4)
            Sbf_p = AB[:, 17536:18048].rearrange("p (b h e) -> p b h e", b=1, h=4)
            Kt = [AB[:, i * 1024:(i + 1) * 1024].rearrange("p (t f) -> p t f", t=2) for i in range(2)]
            Vt = [AB[:, 2048 + i * 1040:2048 + (i + 1) * 1040].rearrange("p (t h e) -> p t h e", t=2, h=4) for i in range(2)]
            KTc = [AB[:, 4128 + i * 1024:4128 + (i + 1) * 1024].rearrange("p (a h t) -> p a h t", a=2, h=4) for i in range(2)]
            qm = AB[:, 6176:8224].rearrange("p (b h t) -> p b h t", b=4, h=4)
            km = AB[:, 8224:10272].rearrange("p (b h t) -> p b h t", b=4, h=4)
            KTs = AB[:, 10272:10784].rearrange("p (h t) -> p h t", h=4)
            VS = AB[:, 10784:11304].rearrange("p (h e) -> p h e", h=4)
            Sbf_s = AB[:, 11304:13352].rearrange("p (b h e) -> p b h e", b=4, h=4)
            Sst_s = ARENA[:, 6676:8724].rearrange("p (b h e) -> p b h e", b=4, h=4)
            xt = [sbt(e1, "xt%d" % i, [128, D], F32) for i in range(2)]
            rt = [sbt(e1, "rt%d" % i, [128, 2, 128], F32) for i in range(2)]
            oT = sbt(e1, "oT", [128, 8, 128], BF16)
            a_bf = oT[:].rearrange("p k t -> p (k t)")
            aT = sbt(e1, "aT", [128, 8, 128], BF16)
            st = sbt(e1, "st", [128, 64], F32)
            krow_t = sbt(e1, "krow", [128, 512], F32)
            krow, vrow, sgd, sgr = krow_t[:], vrow_t[:], krow_t[:], vrow_t[:]
            od_bf_t = sbt(e1, "od_bf", [128, 512], BF16)
            or_bf_t = sbt(e1, "or_bf", [128, 512], BF16)
            od_bf, or_bf, q_bf, k_bf = od_bf_t[:], or_bf_t[:], od_bf_t[:], or_bf_t[:]
            attb = sbt(e1, "attb", [128, 4, 128], BF16)
            qt_bf = attb
            kt_bf = sbt(e1, "kt_bf", [128, 4, 128], BF16)
            qtT = sbt(e1, "qtT", [128, 4, 128], BF16)
            ktT = sbt(e1, "ktT", [128, 4, 128], BF16)
            vr_bf = sbt(e1, "vr_bf", [128, 512], BF16)
            PT = [sbt(e1, "PT%d" % i, [128, 512], BF16) for i in range(3)]
            mixT = sbt(e1, "mixT", [128, 8, 128], BF16)
            junk = mixT[:].rearrange("p k t -> p (k t)")
            qTz = mixT[:].rearrange("p (m h) t -> p m h t", m=2)
            MX = [("mixT", 0), ("mixT", 1)]

            S.ts("dve", gs[:, V_GQ:V_GQ + 64], gs[:, V_GQ:V_GQ + 64], 0.125, 0.0, ALU.mult, ALU.add, reads=["gs"], writes=["gs"])
            S.ts("dve", gs[:, V_GDA:V_GDA + 128], gs[:, V_GDA:V_GDA + 128], 0.8, 0.0, ALU.mult, ALU.add, reads=["gs"], writes=["gs"])
            S.barrier()
            S.memset("pool", VS[:, :, 128:129], 1.0, writes=["VSones"])
            for i in range(2):
                S.memset("pool", Vt[i][:, :, :, 128:129], 1.0, writes=[("Vtones", i)])

            ptc = [0]
            stc = [0]

            def p1_load(kind, s, i, tix):
                sl = tix % 2
                if kind == "s":
                    x_ap, rot_ap = xs, rot[16]
                else:
                    x_ap, rot_ap = xp[s, i * 128:(i + 1) * 128, :], rot[i]
                S.dma("sp", xt[sl][:], x_ap, ("x", sl), writes=[("x", sl)])
                S.dma("sp", rt[sl][:], rot_ap, ("rt", sl), writes=[("rt", sl)])

            fronted = set()

            stated = set()

            def p1_stats(kind, s, i, tix):
                sl = tix % 2
                X = ("x", sl)
                stated.add(tix)
                S.act(od_bf[:], xt[sl][:, 0:512], AF.Square, accum_out=st[:, 0:1], reads=[X], writes=["od_bf", "ss0a"])
                S.act(or_bf[:], xt[sl][:, 512:1024], AF.Square, accum_out=st[:, 3:4], reads=[X], writes=["or_bf", "ss0b"])
                S.tt("dve", st[:, 0:1], st[:, 0:1], st[:, 3:4], ALU.add, reads=["ss0a", "ss0b"], writes=["ss0"])
                S.ts("dve", st[:, 1:2], st[:, 0:1], 1.0 / D, EPS, ALU.mult, ALU.add, reads=["ss0"], writes=["r0_a"])

            def p1_front_a(kind, s, i, tix):
                sl = tix % 2
                X = ("x", sl)
                if tix not in stated:
                    p1_stats(kind, s, i, tix)
                fronted.add(tix)
                S.act(st[:, 1:2], st[:, 1:2], AF.Ln, reads=["r0_a"], writes=["r0_b"])
                S.act(st[:, 2:3], st[:, 1:2], AF.Exp, reads=["r0_b"], writes=["r0"], scale=-0.5)
                S.op("dve", lambda e: e.tensor_scalar_mul(a_bf[:], xt[sl][:], st[:, 2:3]),
                     reads=[X, "r0"], writes=["oT"])

            def p1_tile(kind, s, i, tix, nxt):
                sl = tix % 2
                samp = kind == "s"
                if samp:
                    x_ap, rot_ap = xs, rot[16]
                    k_out, v_out, h_out = kso, vso, h1s
                else:
                    rows = slice(i * 128, (i + 1) * 128)
                    x_ap, rot_ap = xp[s, rows, :], rot[i]
                    k_out, v_out, h_out = kp[s, rows, :], vp[s, rows, :], h1p[s, rows, :]
                X, RT = ("x", sl), ("rt", sl)
                if nxt is not None:
                    p1_load(*nxt)
                if tix not in fronted:
                    p1_front_a(kind, s, i, tix)
                b = gp()
                for k in range(8):
                    S.tr(bankb(b)[:, k * 128:(k + 1) * 128], a_bf[:, k * 128:(k + 1) * 128], idb[:],
                         reads=["oT", "idb"], writes=[pb(b)], sig=(k == 7))
                S.tt("dve", aT[:], bankb(b)[:, 0:1024].rearrange("p (k t) -> p k t", k=8),
                     gfm[:, 0, :].unsqueeze(2).to_broadcast([128, 8, 128]), ALU.mult,
                     reads=[pb(b), "gfm"], writes=["aT"])

                ZB = [6, 7, 2, 3, 4, 5]

                def zchunk(c):
                    b = ZB[c] if c < 6 else gp()
                    for k in range(8):
                        S.mm(bank(b), aT[:, k, :], w_in_sb[:, k, c * 512:(c + 1) * 512], k == 0, k == 7,
                             reads=["aT", W_IN[k]], writes=[pb(b)])
                    return b

                def qknorm(b, goff, tag):
                    S.act(tA[:], bank(b), AF.Square, reads=[pb(b)], writes=["tA"])
                    S.red("dve", st[:, 8:16], tA[:].rearrange("p (a b) -> p a b", a=8), reads=["tA"], writes=["ss8"])
                    rms_rstd(st[:, 8:16], 8, 64, "ss8", "r8", st[:, 16:24], st[:, 24:32])
                    S.tt("dve", tB[:].rearrange("p (a b) -> p a b", a=8), bank(b).rearrange("p (a b) -> p a b", a=8),
                         st[:, 24:32].unsqueeze(2).to_broadcast([128, 8, 64]), ALU.mult,
                         reads=[pb(b), "r8"], writes=["tB"])

                zb = [zchunk(c) for c in range(6)]
                zb6 = zchunk(6)
                b = zb[0]
                qknorm(b, V_GQ, "q")
                S.tt("dve", q_bf[:].rearrange("p (a b) -> p a b", a=8), tB[:].rearrange("p (a b) -> p a b", a=8),
                     gs[:, V_GQ:V_GQ + 64].unsqueeze(1).to_broadcast([128, 8, 64]), ALU.mult,
                     reads=["tB", "gs"], writes=["od_bf"])
                b = zb[1]
                qknorm(b, V_GK, "k")
                S.tt("dve", krow[:].rearrange("p (a b) -> p a b", a=8), tB[:].rearrange("p (a b) -> p a b", a=8),
                     gs[:, V_GK:V_GK + 64].unsqueeze(1).to_broadcast([128, 8, 64]), ALU.mult,
                     reads=["tB", "gs"], writes=["krow"])
                S.dma("sp", k_out, krow[:], "ko", reads=["krow"])
                S.cp("pool", k_bf[:], krow[:], reads=["krow"], writes=["or_bf"])
                b = gp()
                for h in range(4):
                    S.tr(bankb(b)[:, h * 128:(h + 1) * 128], q_bf[:, h * 128:(h + 1) * 128], idb[:],
                         reads=["od_bf", "idb"], writes=[pb(b)], sig=False)
                for h in range(4):
                    S.tr(bankb(b)[:, 512 + h * 128:512 + (h + 1) * 128], k_bf[:, h * 128:(h + 1) * 128], idb[:],
                         reads=["or_bf", "idb"], writes=[pb(b)], sig=(h == 3))
                S.memset("pool", mixT[:], 0.0, writes=MX)
                S.cp("act", qTz[0:64, 0], v4(bankb(b)[0:64, 0:512]), reads=[pb(b)] + MX, writes=MX)
                S.cp("act", qTz[64:128, 1], v4(bankb(b)[64:128, 0:512]), reads=[pb(b)] + MX, writes=MX)
                if samp:
                    S.cp("act", KTs[:].rearrange("p h t -> p (h t)"), bankb(b)[:, 512:1024], reads=[pb(b)], writes=["KTs"])
                else:
                    S.cp("act", KT[:, :, i * 128:(i + 1) * 128], v4(bankb(b)[:, 512:1024]), reads=[pb(b)], writes=[("KT", i)])
                b = zb[2]
                S.cp("act", vrow[:], bank(b), reads=[pb(b)], writes=["vrow"])
                S.dma("sp", v_out, vrow[:], "vo", reads=["vrow"])
                if samp:
                    S.cp("pool", VS[:, :, 0:128], v4(vrow[:]), reads=["vrow", "VSones"], writes=["VS"])
                else:
                    S.cp("pool", VC[:, i, :, 0:128], v4(vrow[:]), reads=["vrow", "VCones"], writes=[("VC", i)])

                def rotary(c, qk, out_bf, tagout):
                    b = zb[c]
                    dq0 = C_DQK + (8 if samp else 0) + qk * 4
                    S.tt("dve", v4(tA[:]), v4(bank(b)), cst_sb[:, dq0:dq0 + 4].unsqueeze(2).to_broadcast([128, 4, 128]), ALU.mult,
                         reads=[pb(b), "cst"], writes=["tA"])
                    CC = rt[sl][:, 0, :].unsqueeze(1).to_broadcast([128, 4, 128])
                    S.tt("pool", v4(tB[:]), v4(tA[:]), CC, ALU.mult, reads=["tA", RT], writes=["tB"])
                    S.tt("pool", v4(tC[:])[:, :, 0:64], v4(tA[:])[:, :, 64:128],
                         rt[sl][:, 1, 0:64].unsqueeze(1).to_broadcast([128, 4, 64]), ALU.mult,
                         reads=["tA", RT], writes=["tC0"])
                    S.tt("pool", v4(tC[:])[:, :, 64:128], v4(tA[:])[:, :, 0:64],
                         rt[sl][:, 1, 64:128].unsqueeze(1).to_broadcast([128, 4, 64]), ALU.mult,
                         reads=["tA", RT], writes=["tC1"])
                    S.tt("dve", out_bf[:].rearrange("p h d -> p (h d)"), tB[:], tC[:], ALU.add,
                         reads=["tB", "tC0", "tC1"], writes=[tagout])

                rotary(3, 0, qt_bf, "attb")
                rotary(4, 1, kt_bf, "kt_bf")
                b = zb[5]
                S.cp("act", vr_bf[:], bank(b), reads=[pb(b)], writes=["vr_bf"])
                b = zb6
                S.act(sg[:], bank(b), AF.Silu, reads=[pb(b)], writes=["sg"])

                OB = [2, 3, 4, 5]
                if not samp:
                    groups = []
                    for h in range(4):
                        js = list(range(i + 1))
                        for g0 in range(0, len(js), 2):
                            groups.append((h, js[g0:g0 + 2]))

                    def emit_st(g):
                        h, js = g
                        sb_ = 6 + (stc[0] % 2)
                        stc[0] += 1
                        for jj, j in enumerate(js):
                            band = j >= i - 1
                            kd = 0 if j == i else 1
                            o = bank(sb_)[:, jj * 256:(jj + 1) * 256].rearrange("p (m t) -> p m t", m=2)
                            S.mm(o, KT[:, h, j * 128:(j + 1) * 128], qTz[:, :, h, :],
                                 True, not band, reads=[("KT", j)] + MX, writes=[pb(sb_)])
                            if band:
                                S.mm(o, idb[:], DH[:, kd, h, :].unsqueeze(1).to_broadcast([128, 2, 128]), False, False,
                                     reads=["idb", "DH"], writes=[pb(sb_)])
                                S.mm(o, idb[:], DL[:, kd, h, :].unsqueeze(1).to_broadcast([128, 2, 128]), False, True,
                                     reads=["idb", "DL"], writes=[pb(sb_)])
                        return sb_

                    def emit_exp_pv(g, sb_):
                        h, js = g
                        w = len(js) * 256
                        pt = ptc[0] % 3
                        ptc[0] += 1
                        S.act(PT[pt][:, 0:w], bank(sb_)[:, 0:w], AF.Exp, reads=[pb(sb_)], writes=[("PT", pt)])
                        for jj, j in enumerate(js):
                            for m in range(2):
                                S.mm(bank(OB[h])[:, m * 129:(m + 1) * 129], PT[pt][:, (jj * 2 + m) * 128:(jj * 2 + m + 1) * 128],
                                     VC[:, j, h, 0:129], (j == 0 and m == 0), j == i,
                                     reads=[("PT", pt), ("VC", j), "VCones"], writes=[pb(OB[h])],
                                     sig=(j == i and m == 1), skip_group_check=True)

                    prev = emit_st(groups[0])
                    for gi, g in enumerate(groups):
                        nxt = emit_st(groups[gi + 1]) if gi + 1 < len(groups) else None
                        emit_exp_pv(g, prev)
                        prev = nxt
                else:
                    for hp in range(2):
                        sb_ = 6 + (stc[0] % 2)
                        stc[0] += 1
                        for hh in range(2):
                            h = hp * 2 + hh
                            o = bank(sb_)[:, hh * 256:(hh + 1) * 256].rearrange("p (m t) -> p m t", m=2)
                            S.mm(o, KTs[:, h, :], qTz[:, :, h, :], True, False,
                                 reads=["KTs"] + MX, writes=[pb(sb_)])
                            S.mm(o, idb[:], DH[:, 2, h, :].unsqueeze(1).to_broadcast([128, 2, 128]), False, False,
                                 reads=["idb", "DH"], writes=[pb(sb_)])
                            S.mm(o, idb[:], DL[:, 2, h, :].unsqueeze(1).to_broadcast([128, 2, 128]), False, True,
                                 reads=["idb", "DL"], writes=[pb(sb_)])
                        pt = ptc[0] % 3
                        ptc[0] += 1
                        S.act(PT[pt][:], bank(sb_), AF.Exp, reads=[pb(sb_)], writes=[("PT", pt)])
                        for hh in range(2):
                            h = hp * 2 + hh
                            for m in range(2):
                                S.mm(bank(OB[h])[:, m * 129:(m + 1) * 129], PT[pt][:, (hh * 2 + m) * 128:(hh * 2 + m + 1) * 128],
                                     VS[:, h, 0:129], m == 0, False, reads=[("PT", pt), "VS", "VSones"], writes=[pb(OB[h])],
                                     sig=True, skip_group_check=True)
                    ldc = [0]
                    for bq in range(4 if "cache" not in _SKIP else 0):
                        for j2 in range(16):
                            ls = ldc[0] % 2
                            ldc[0] += 1
                            S.dma("pool", Kt[ls], ck[bq, j2 * 256:(j2 + 1) * 256, :].rearrange("(t p) f -> p t f", p=128),
                                  ("Kt", ls), writes=[("Kt", ls)])
                            for t_ in range(2):
                                S.dma("pool", Vt[ls][:, t_, :, 0:128],
                                      cv[bq, j2 * 256 + t_ * 128:j2 * 256 + (t_ + 1) * 128, :].rearrange("p (h e) -> p h e", h=4),
                                      ("Vt", ls, t_), reads=[("Vtones", ls)], writes=[("Vt", ls, t_)])
                            ks_ = ls
                            if "cpe" in _SKIP:
                                continue
                            b = gp()
                            for jj in range(2):
                                for h in range(4):
                                    S.tr(bankb(b)[:, (jj * 4 + h) * 128:(jj * 4 + h + 1) * 128], Kt[ls][:, jj, h * 128:(h + 1) * 128],
                                         idb[:], reads=[("Kt", ls), "idb"], writes=[pb(b)], sig=(jj == 1 and h == 3))
                            S.cp("act", KTc[ks_].rearrange("p a h t -> p (a h t)"), bankb(b)[:, 0:1024],
                                 reads=[pb(b)], writes=[("KTc", ks_)])
                            if "cst" in _SKIP:
                                continue
                            sb_ = 6 + (stc[0] % 2)
                            stc[0] += 1
                            for jj in range(2):
                                j = j2 * 2 + jj
                                band = j == 31
                                for h in range(4):
                                    c0 = (jj * 4 + h) * 64
                                    o = bank(sb_)[:, c0:c0 + 64].rearrange("p (m t) -> p m t", m=2)
                                    S.mm(o, KTc[ks_][:, jj, h, :], qTz[:, :, h, bq * 32:(bq + 1) * 32],
                                         True, not band, reads=[("KTc", ks_)] + MX, writes=[pb(sb_)],
                                         sig=(not band and jj == 1 and h == 3))
                                    if band:
                                        S.mm(o, idb[:], DH[:, 1, h, 0:32].unsqueeze(1).to_broadcast([128, 2, 32]), False, False,
                                             reads=["idb", "DH"], writes=[pb(sb_)])
                                        S.mm(o, idb[:], DL[:, 1, h, 0:32].unsqueeze(1).to_broadcast([128, 2, 32]), False, True,
                                             reads=["idb", "DL"], writes=[pb(sb_)])
                            pt = ptc[0] % 3
                            ptc[0] += 1
                            S.act(PT[pt][:], bank(sb_), AF.Exp, reads=[pb(sb_)], writes=[("PT", pt)])
                            for jj in range(2):
                                j = j2 * 2 + jj
                                for h in range(4):
                                    for m in range(2):
                                        c0 = ((jj * 4 + h) * 2 + m) * 32
                                        S.mm(bank(OB[h])[bq * 32:(bq + 1) * 32, m * 129:(m + 1) * 129], PT[pt][:, c0:c0 + 32],
                                             Vt[ls][:, jj, h, 0:129], False, j == 31,
                                             reads=[("PT", pt), ("Vt", ls, 0), ("Vt", ls, 1), ("Vtones", ls)], writes=[pb(OB[h])],
                                             sig=(jj == 1 and h == 3 and m == 1), skip_group_check=True,
                                             tile_position=(0, bq * 32))
                b = gp()
                for h in range(4):
                    S.tr(bankb(b)[:, h * 128:(h + 1) * 128], qt_bf[:, h, :], idb[:],
                         reads=["attb", "idb"], writes=[pb(b)], sig=False)
                for h in range(4):
                    S.tr(bankb(b)[:, 512 + h * 128:512 + (h + 1) * 128], kt_bf[:, h, :], idb[:],
                         reads=["kt_bf", "idb"], writes=[pb(b)], sig=(h == 3))
                S.cp("act", qtT[:].rearrange("p h t -> p (h t)"), bankb(b)[:, 0:512], reads=[pb(b)], writes=["qtT"])
                S.cp("act", ktT[:].rearrange("p h t -> p (h t)"), bankb(b)[:, 512:1024], reads=[pb(b)], writes=["ktT"])
                Ov = PS[:, 1024:3072].rearrange("p (h x) -> p h x", h=4)
                OBR = [pb(k) for k in OB]
                Osum = Ov[:, :, 0:258].rearrange("p h (m e) -> p h m e", m=2)[:, :, :, 128]
                S.recip(st[:, 32:40].rearrange("p (h m) -> p h m", h=4), Osum, reads=OBR, writes=["rec"])
                recv = st[:, 32:40].rearrange("p (h m) -> p h m", h=4)
                S.tt("dve", st[:, 40:44], recv[:, :, 1], NLAM.to_broadcast([128, 4]), ALU.mult, reads=["rec", "lam"], writes=["rec2"])
                S.tt("dve", v4(tA[:]), Ov[:, :, 0:128], recv[:, :, 0:1].to_broadcast([128, 4, 128]), ALU.mult,
                     reads=OBR + ["rec"], writes=["tA"])
                S.tt("dve", v4(tB[:]), Ov[:, :, 129:257], st[:, 40:44].unsqueeze(2).to_broadcast([128, 4, 128]), ALU.mult,
                     reads=OBR + ["rec2"], writes=["tB"])
                S.tt("pool", od[:], tA[:], tB[:], ALU.add, reads=["tA", "tB"], writes=["vrow"])
                S.act(tC[:], od[:], AF.Square, reads=["vrow"], writes=["tC0", "tC1"])
                S.red("dve", st[:, 44:48], v4(tC[:]), reads=["tC0", "tC1"], writes=["ss4"])
                rms_rstd(st[:, 44:48], 4, 128, "ss4", "r4", st[:, 48:52], st[:, 52:56])
                S.tt("dve", v4(tA[:]), v4(od[:]), st[:, 52:56].unsqueeze(2).to_broadcast([128, 4, 128]), ALU.mult,
                     reads=["vrow", "r4"], writes=["tA"])
                S.tt("pool", v4(od_bf[:]), v4(tA[:]), gs[:, V_GDA:V_GDA + 128].unsqueeze(1).to_broadcast([128, 4, 128]), ALU.mult,
                     reads=["tA", "gs"], writes=["od_bf"])

                nsq = 4 if samp else 1
                first = samp or i == 0
                gam = cst_sb[:, C_GAM + (4 if samp else 0):C_GAM + (4 if samp else 0) + 4]
                rmoff = C_RMS if samp else C_RMP
                Sst = Sst_s if samp else Sst_p
                Sbf = Sbf_s if samp else Sbf_p
                if samp:
                    if "sret" not in _SKIP:
                        S.dma("sp", Sst, sret.rearrange("b h d e -> d b h e"), "sst", writes=["Sst"])
                        S.dma("pool", Sbf, sret.rearrange("b h d e -> d b h e"), "sbf", writes=["Sbf"])
                    for bq in range(4):
                        S.tt("dve", qm[:, bq], qtT[:], smq[:, bq, :].unsqueeze(1).to_broadcast([128, 4, 128]), ALU.mult,
                             reads=["qtT", "smq"], writes=["qm"])
                        S.tt("dve", km[:, bq].rearrange("p h d -> p (h d)"), kt_bf[:].rearrange("p h d -> p (h d)"),
                             cst_sb[:, C_KM + bq:C_KM + bq + 1].to_broadcast([128, 512]), ALU.mult,
                             reads=["kt_bf", "cst"], writes=[("km", bq)])
                b1 = gp()
                for h in range(4):
                    S.mm(bank(b1)[:, h * 128:(h + 1) * 128], ktT[:, h, :], qtT[:, h, :], True, True,
                         reads=["ktT", "qtT"], writes=[pb(b1)], sig=(h == 3))
                S.tt("dve", attb[:], v4(bank(b1)), cst_sb[:, rmoff:rmoff + 128].unsqueeze(1).to_broadcast([128, 4, 128]), ALU.mult,
                     reads=[pb(b1), "cst"], writes=["attb"])
                for gb_, goff in ((6, 3584), (7, 4608)):
                    for f in range(4):
                        for k in range(8):
                            S.mm(bank(gb_)[:, f * 128:(f + 1) * 128], w_in_sb[:, k, goff + f * 128:goff + (f + 1) * 128], aT[:, k, :],
                                 k == 0, k == 7, reads=["aT", W_IN[k]], writes=[pb(gb_)], sig=(k == 7 and f == 3))
                b2 = gp()
                for h in range(4):
                    o = bank(b2)[:, h * 128:(h + 1) * 128]
                    cross = samp or i > 0
                    S.mm(o, attb[:, h, :], vr_bf[:, h * 128:(h + 1) * 128], True, not cross,
                         reads=["attb", "vr_bf"], writes=[pb(b2)], sig=(not cross and h == 3))
                    if cross:
                        for bq in range(nsq):
                            lhs = qm[:, bq, h, :] if samp else qtT[:, h, :]
                            S.mm(o, lhs, Sbf[:, bq, h, :], False, bq == nsq - 1,
                                 reads=["qm" if samp else "qtT", "Sbf"], writes=[pb(b2)], sig=(bq == nsq - 1 and h == 3))
                S.act(tC[:], bank(b2), AF.Square, reads=[pb(b2)], writes=["tC0", "tC1"])
                S.red("dve", st[:, 56:60], v4(tC[:]), reads=["tC0", "tC1"], writes=["ss4r"])
                rms_rstd(st[:, 56:60], 4, 128, "ss4r", "r4r", st[:, 60:64], st[:, 4:8])
                S.tt("dve", v4(tA[:]), v4(bank(b2)), st[:, 4:8].unsqueeze(2).to_broadcast([128, 4, 128]), ALU.mult,
                     reads=[pb(b2), "r4r"], writes=["tA"])
                S.tt("pool", v4(tB[:]), v4(tA[:]), gs[:, V_GRT:V_GRT + 128].unsqueeze(1).to_broadcast([128, 4, 128]), ALU.mult,
                     reads=["tA", "gs"], writes=["tB"])
                S.tt("pool", or_bf[:], tB[:], sg[:], ALU.mult, reads=["tB", "sg"], writes=["or_bf"])
                last_tile = samp or i == 15
                need_state = samp or True
                for bq in range(nsq):
                    b3 = gp()
                    for h in range(4):
                        lhs = km[:, bq, h, :] if samp else kt_bf[:, h, :]
                        S.mm(bank(b3)[:, h * 128:(h + 1) * 128], lhs, vr_bf[:, h * 128:(h + 1) * 128], True, True,
                             reads=[("km", bq) if samp else "kt_bf", "vr_bf"], writes=[pb(b3)], sig=(h == 3))
                    gb = gam.unsqueeze(2).to_broadcast([128, 4, 128])
                    if first and not samp:
                        S.tt("dve", Sst[:, bq], v4(bank(b3)), gb, ALU.mult, reads=[pb(b3), "cst"], writes=["Sst"])
                    else:
                        S.tt("dve", Sst[:, bq], v4(bank(b3)), Sst[:, bq], ALU.add, reads=[pb(b3), "Sst"], writes=["Sst"])
                        S.tt("pool", Sst[:, bq], Sst[:, bq], gb, ALU.mult, reads=["Sst", "cst"], writes=["Sst"])
                    if not samp:
                        S.cp("pool", Sbf[:, bq], Sst[:, bq], reads=["Sst"], writes=["Sbf"])
                if samp:
                    if "sret" not in _SKIP:
                        S.dma("sp", rso.rearrange("b h d e -> d b h e"), Sst, "rso", reads=["Sst"])
                elif i == 15:
                    S.dma("sp", rp[s].rearrange("h d e -> d h e"), Sst[:, 0], "rpo", reads=["Sst"])

                for f in range(4):
                    fc = 4 + f
                    for k in range(8):
                        S.mm(bank(2)[:, f * 128:(f + 1) * 128], w_in_sb[:, k, 3584 + fc * 128:3584 + (fc + 1) * 128], aT[:, k, :],
                             k == 0, k == 7, reads=["aT", W_IN[k]], writes=[pb(2)], sig=(k == 7 and f == 3))
                for f in range(4):
                    fc = 4 + f
                    for k in range(8):
                        S.mm(bank(3)[:, f * 128:(f + 1) * 128], w_in_sb[:, k, 4608 + fc * 128:4608 + (fc + 1) * 128], aT[:, k, :],
                             k == 0, k == 7, reads=["aT", W_IN[k]], writes=[pb(3)], sig=(k == 7 and f == 3))
                b = gp()
                for h in range(4):
                    S.tr(bankb(b)[:, h * 128:(h + 1) * 128], od_bf[:, h * 128:(h + 1) * 128], idb[:],
                         reads=["od_bf", "idb"], writes=[pb(b)], sig=False)
                for h in range(4):
                    S.tr(bankb(b)[:, 512 + h * 128:512 + (h + 1) * 128], or_bf[:, h * 128:(h + 1) * 128], idb[:],
                         reads=["or_bf", "idb"], writes=[pb(b)], sig=(h == 3))
                S.cp("act", oT[:].rearrange("p k t -> p (k t)"), bankb(b)[:, 0:1024], reads=[pb(b)], writes=["oT"])
                if nxt is not None:
                    p1_stats(*nxt)
                for r in range(2):
                    gd_, gr_ = (6, 7) if r == 0 else (2, 3)
                    for f in range(0):
                        fc = r * 4 + f
                        for k in range(8):
                            S.mm(bank(2)[:, f * 128:(f + 1) * 128], w_in_sb[:, k, 3584 + fc * 128:3584 + (fc + 1) * 128], aT[:, k, :],
                                 k == 0, k == 7, reads=["aT", W_IN[k]], writes=[pb(2)], sig=(k == 7 and f == 3))
                    for f in range(0):
                        fc = r * 4 + f
                        for k in range(8):
                            S.mm(bank(3)[:, f * 128:(f + 1) * 128], w_in_sb[:, k, 4608 + fc * 128:4608 + (fc + 1) * 128], aT[:, k, :],
                                 k == 0, k == 7, reads=["aT", W_IN[k]], writes=[pb(3)], sig=(k == 7 and f == 3))
                    for f in range(4):
                        fc = r * 4 + f
                        for k in range(4):
                            S.mm(bank(4)[:, f * 128:(f + 1) * 128], w_bd_sb[:, k, fc * 128:(fc + 1) * 128], oT[:, k, :],
                                 k == 0, k == 3, reads=["oT", "w_bd"], writes=[pb(4)], sig=(k == 3 and f == 3))
                    for f in range(4):
                        fc = r * 4 + f
                        for k in range(4):
                            S.mm(bank(5)[:, f * 128:(f + 1) * 128], w_br_sb[:, k, fc * 128:(fc + 1) * 128], oT[:, 4 + k, :],
                                 k == 0, k == 3, reads=["oT", "w_br"], writes=[pb(5)], sig=(k == 3 and f == 3))
                    S.act(sgd[:], bank(gd_), AF.Sigmoid, reads=[pb(gd_)], writes=["krow"])
                    S.act(sgr[:], bank(gr_), AF.Sigmoid, reads=[pb(gr_)], writes=["vrow"])
                    S.tt("dve", tA[:], bank(4), sgd[:], ALU.mult, reads=[pb(4), "krow"], writes=["tA"])
                    S.tt("dve", tB[:], bank(5), sgr[:], ALU.mult, reads=[pb(5), "vrow"], writes=["tB"])
                    S.tt("pool", mixT[:, r * 4:(r + 1) * 4, :].rearrange("p k t -> p (k t)"), tA[:], tB[:], ALU.add,
                         reads=["tA", "tB"], writes=[("mixT", r)])
                if nxt is not None:
                    p1_front_a(*nxt)
                for n in range(2):
                    b = gp()
                    for fc in range(8):
                        S.mm(bank(b), mixT[:, fc, :], w_o_sb[:, fc, n * 512:(n + 1) * 512], fc == 0, fc == 7,
                             reads=[("mixT", fc // 4), "w_o"], writes=[pb(b)])
                    S.tt("dve", xt[sl][:, n * 512:(n + 1) * 512], bank(b), xt[sl][:, n * 512:(n + 1) * 512], ALU.add,
                         reads=[pb(b), X], writes=[X])
                S.dma("sp", h_out, xt[sl][:], "h1o", reads=[X])

            tl = ([("s", 0, 0)] if do_sample else []) + [("p", s_, i_) for s_ in range(2) for i_ in range(ntp)]
            tl = [t + (k,) for k, t in enumerate(tl)]
            if not do_sample:
                S.memset("pool", VC[:, :, :, 128:129], 1.0, writes=["VCones"])
            if tl:
                p1_load(*tl[0])
            for k, t in enumerate(tl):
                p1_tile(*t, tl[k + 1] if k + 1 < len(tl) else None)
                if t[0] == "s":
                    S.barrier()
                    S.memset("pool", VC[:, :, :, 128:129], 1.0, writes=["VCones"])

        S.barrier()
        if do_p2:
            with ExitStack() as e2:
                w_g_sb = sbt(e2, "w_g_sb", [128, 8, DFF], BF16)
                w_u_sb = sbt(e2, "w_u_sb", [128, 8, DFF], BF16)
                w_d_sb = sbt(e2, "w_d_sb", [128, NCC, D], BF16)
                w_pe_sb = sbt(e2, "w_pe_sb", [128, 2, D], BF16)
                w_pg_sb = sbt(e2, "w_pg_sb", [128, 8, D], BF16)
                wgv = w_g.rearrange("(k p) n -> p k n", p=128)
                wuv = w_u.rearrange("(k p) n -> p k n", p=128)
                for k in range(8):
                    S.dma("pool", w_g_sb[:, k, :], wgv[:, k, :], ("w_g", k), writes=[("w_g", k)])
                    S.dma("pool", w_u_sb[:, k, :], wuv[:, k, :], ("w_u", k), writes=[("w_u", k)])
                wdv = w_d.rearrange("(k p) n -> p k n", p=128)
                for c0 in range(0, NCC, 8):
                    c1 = min(NCC, c0 + 8)
                    S.dma("pool", w_d_sb[:, c0:c1, :], wdv[:, c0:c1, :], ("w_d", c0), writes=[("w_d", c0)])
                WD = [("w_d", (c // 8) * 8) for c in range(NCC)]
                S.dma("pool", w_pe_sb[:], w_pe.rearrange("(k p) n -> p k n", p=128), "w_pe", writes=["w_pe"])
                S.dma("pool", w_pg_sb[:], w_pg.rearrange("(k p) n -> p k n", p=128), "w_pg", writes=["w_pg"])

                ht = [sbt(e2, "ht%d" % i, [128, D], F32) for i in range(2)]
                p_bf = [sbt(e2, "p_bf%d" % i, [128, 256], BF16) for i in range(2)]
                f_bf = sbt(e2, "f_bf", [128, D], BF16)
                fT = sbt(e2, "fT", [128, 8, 128], BF16)
                f3T = sbt(e2, "f3T", [128, 8, 128], BF16)
                pT = sbt(e2, "pT", [128, 2, 128], BF16)
                ft8 = sbt(e2, "ft8", [128, 8, 8], BF16)
                st2 = sbt(e2, "st2", [128, 16], F32)
                Gt = sbt(e2, "Gt", [128, NCC * 4 * 34], F32)
                Gp = Gt[:, 0:NCC * 130].rearrange("p (c l) -> p c l", c=NCC)
                Gs = Gt[:].rearrange("p (c b l) -> p c b l", c=NCC, b=4)
                C4 = sbt(e2, "C4", [128, 4, 512], F32)
                cA, cB, cC, glu = C4[:, 0, :], C4[:, 1, :], C4[:, 2, :], C4[:, 3, :]
                sg3 = C4[:, 0:2, :].rearrange("p a b -> p (a b)")
                hhT = sbt(e2, "hhT", [128, NCC, 128], BF16)
                yt_t = sbt(e2, "yt", [128, D], F32)
                yt = yt_t[:]
                f3_bf = sbt(e2, "f3_bf", [128, D], BF16)
                sg3h = sbt(e2, "sg3h", [128, 512], F32)
                cnv = [sbt(e2, "cnv%d" % i, [8, 512], F32) for i in range(1)]

                def p2_load(kind, s, i, tix):
                    sl = tix % 2
                    if kind == "s":
                        h_in, p_ap = h1s, psm
                    else:
                        rows = slice(i * 128, (i + 1) * 128)
                        h_in, p_ap = h1p[s, rows, :], pp[s, rows, :]
                    S.dma("sp", ht[sl][:], h_in, ("ht", sl), writes=[("ht", sl)])
                    S.dma("pool", p_bf[sl][:], p_ap, ("p_bf", sl), writes=[("p_bf", sl)])

                def p2_front_a(kind, s, i, tix):
                    sl = tix % 2
                    H = ("ht", sl)
                    S.act(f_bf[:], ht[sl][:], AF.Square, accum_out=st2[:, 0:1], reads=[H], writes=["f_bf", "q0"])
                    rms_rstd(st2[:, 0:1], 1, D, "q0", "q2", st2[:, 1:2], st2[:, 2:3])
                    S.op("dve", lambda e: e.tensor_scalar_mul(f_bf[:], ht[sl][:], st2[:, 2:3]),
                         reads=[H, "q2"], writes=["f_bf"])

                def p2_front_b(kind, s, i, tix):
                    b = gp()
                    for k in range(8):
                        S.tr(bankb(b)[:, k * 128:(k + 1) * 128], f_bf[:, k * 128:(k + 1) * 128], idb[:],
                             reads=["f_bf", "idb"], writes=[pb(b)], sig=(k == 7))
                    S.tt("dve", fT[:], bankb(b)[:, 0:1024].rearrange("p (k t) -> p k t", k=8),
                         gfm[:, 1, :].unsqueeze(2).to_broadcast([128, 8, 128]), ALU.mult, reads=[pb(b), "gfm"], writes=["fT"])

                def p2_mid(kind, s, i, tix, filler, after_filler):
                    sl = tix % 2
                    samp = kind == "s"
                    H = ("ht", sl)
                    fstate = [filler, False]

                    def step():
                        if fstate[0] is not None:
                            try:
                                next(fstate[0])
                            except StopIteration:
                                fstate[0] = None
                        if fstate[0] is None and not fstate[1]:
                            fstate[1] = True
                            after_filler()

                    if filler is None:
                        step()
                    if samp:
                        S.dma("sp", Gs[:, :, :, 0:2], scv, "scv", writes=["Gc"])
                    elif i == 0:
                        S.memset("pool", Gp[:, :, 0:2], 0.0, writes=["Gc"])
                    gi = 0
                    for c0 in range(0, NCC, 4):
                        n = min(4, NCC - c0)
                        w = n * 128
                        bg, bu = 2 + (gi % 2), 4 + (gi % 2)
                        gi += 1
                        for f in range(n):
                            cc = c0 + f
                            for k in range(8):
                                S.mm(bank(bg)[:, f * 128:(f + 1) * 128], w_g_sb[:, k, cc * 128:(cc + 1) * 128], fT[:, k, :],
                                     k == 0, k == 7, reads=["fT", ("w_g", k)], writes=[pb(bg)], sig=(k == 7 and f == n - 1))
                        for f in range(n):
                            cc = c0 + f
                            for k in range(8):
                                S.mm(bank(bu)[:, f * 128:(f + 1) * 128], w_u_sb[:, k, cc * 128:(cc + 1) * 128], fT[:, k, :],
                                     k == 0, k == 7, reads=["fT", ("w_u", k)], writes=[pb(bu)], sig=(k == 7 and f == n - 1))
                        if samp:
                            gsrc = bank(bg)[:, 0:w].rearrange("p (c b l) -> p c b l", c=n, b=4)
                            gdst = Gs[:, c0:c0 + n, :, 2:34]

                            def tap(j):
                                return Gs[:, c0:c0 + n, :, j:j + 32]

                            def shp(ap):
                                return ap.rearrange("p (c b l) -> p c b l", c=n, b=4)

                            def wb_(j):
                                return cw_sb[:, j, c0:c0 + n].unsqueeze(2).unsqueeze(3).to_broadcast([128, n, 4, 32])
                        else:
                            gsrc = bank(bg)[:, 0:w].rearrange("p (c l) -> p c l", c=n)
                            gdst = Gp[:, c0:c0 + n, 2:130]

                            def tap(j):
                                return Gp[:, c0:c0 + n, j:j + 128]

                            def shp(ap):
                                return ap.rearrange("p (c l) -> p c l", c=n)

                            def wb_(j):
                                return cw_sb[:, j, c0:c0 + n].unsqueeze(2).to_broadcast([128, n, 128])
                        GR = ("G", c0)
                        S.cp("act", gdst, gsrc, reads=[pb(bg), "Gc"], writes=[GR])
                        ab = gi % 2
                        CA, GL = ("cacc", ab), ("glu", ab)
                        acc_t = C4[:, ab, :]
                        glu_t = C4[:, 2 + ab, :]
                        accs, taps = [], []
                        for f in range(n):
                            cc = c0 + f
                            if samp:
                                acc = acc_t[:, f * 128:(f + 1) * 128].rearrange("p (b l) -> p b l", b=4)
                                gps = bank(bg)[:, f * 128:(f + 1) * 128].rearrange("p (b l) -> p b l", b=4)
                                tp = (Gs[:, cc, :, 0:32], Gs[:, cc, :, 1:33])
                            else:
                                acc = acc_t[:, f * 128:(f + 1) * 128]
                                gps = bank(bg)[:, f * 128:(f + 1) * 128]
                                tp = (Gp[:, cc, 0:128], Gp[:, cc, 1:129])
                            accs.append(acc)
                            taps.append(tp)
                            S.act(acc, gps, AF.Identity, scale=cw_sb[:, 2, cc:cc + 1], bias=cw_sb[:, 3, cc:cc + 1],
                                  reads=[pb(bg), "cw"], writes=[(CA, f)])
                        for j in (1, 0):
                            for f in range(n):
                                cc = c0 + f
                                eng = "dve"
                                S.op(eng, lambda e, acc=accs[f], t=taps[f][j], cc=cc, j=j: e.scalar_tensor_tensor(
                                    acc, t, cw_sb[:, j, cc:cc + 1], acc, ALU.mult, ALU.add),
                                    reads=[GR, "Gc", "cw", (CA, f)], writes=[(CA, f)])
                        CAs = [(CA, f) for f in range(n)]
                        S.act(glu_t[:, 0:w], acc_t[:, 0:w], AF.Gelu_apprx_tanh, reads=CAs, writes=[GL] + [(CA, f) for f in range(4)])
                        S.tt("dve", hhT[:, c0:c0 + n, :].rearrange("p c t -> p (c t)"), bank(bu)[:, 0:w], glu_t[:, 0:w], ALU.mult,
                             reads=[pb(bu), GL], writes=[("hhT", c0)])
                        step()
                    while fstate[0] is not None or not fstate[1]:
                        step()
                    GALL = [("G", c0) for c0 in range(0, NCC, 4)]
                    if not samp and i < 15:
                        S.cp("pool", Gp[:, :, 0:2], Gp[:, :, 128:130], reads=GALL, writes=["Gc"] + GALL)
                    if samp or i == 15:
                        if samp:
                            S.cp("dve", ft8[:].rearrange("p k (b j) -> p k b j", b=4),
                                 fT[:].rearrange("p k (b l) -> p k b l", b=4)[:, :, :, 30:32], reads=["fT"], writes=["ft8"])
                            M = 8
                        else:
                            S.cp("dve", ft8[:, :, 0:2], fT[:, :, 126:128], reads=["fT"], writes=["ft8"])
                            M = 2
                        for c0 in range(0, DFF, 512):
                            w = min(512, DFF - c0)
                            b = gp()
                            for k in range(8):
                                S.mm(bank(b)[0:M, 0:w], ft8[:, k, 0:M], w_g_sb[:, k, c0:c0 + w], k == 0, k == 7,
                                     reads=["ft8", ("w_g", k)], writes=[pb(b)])
                            ci = 0
                            S.cp("act", cnv[ci][0:M, 0:w], bank(b)[0:M, 0:w], reads=[pb(b)], writes=[("cnv", ci)])
                            dst = cso.rearrange("b j c -> (b j) c") if samp else cpo[s]
                            S.dma("sp", dst[:, c0:c0 + w], cnv[ci][0:M, 0:w], ("cnvo", ci), reads=[("cnv", ci)])
                    yield
                    for n_ in range(2):
                        b = gp()
                        for cc in range(NCC):
                            S.mm(bank(b), hhT[:, cc, :], w_d_sb[:, cc, n_ * 512:(n_ + 1) * 512], cc == 0, cc == NCC - 1,
                                 reads=[("hhT", (cc // 4) * 4), WD[cc]], writes=[pb(b)])
                        S.tt("dve", ht[sl][:, n_ * 512:(n_ + 1) * 512], bank(b), ht[sl][:, n_ * 512:(n_ + 1) * 512], ALU.add,
                             reads=[pb(b), H], writes=[H])

                def p2_back(kind, s, i, tix):
                    sl = tix % 2
                    samp = kind == "s"
                    y_out = ys if samp else yp[s, i * 128:(i + 1) * 128, :]
                    H, Pp = ("ht", sl), ("p_bf", sl)
                    S.act(f3_bf[:], ht[sl][:], AF.Square, accum_out=st2[:, 4:5], reads=[H], writes=["f3_bf", "q4"])
                    rms_rstd(st2[:, 4:5], 1, D, "q4", "q6", st2[:, 5:6], st2[:, 6:7])
                    S.op("dve", lambda e: e.tensor_scalar_mul(f3_bf[:], ht[sl][:], st2[:, 6:7]),
                         reads=[H, "q6"], writes=["f3_bf"])
                    yield
                    b = gp()
                    for k in range(8):
                        S.tr(bankb(b)[:, k * 128:(k + 1) * 128], f3_bf[:, k * 128:(k + 1) * 128], idb[:],
                             reads=["f3_bf", "idb"], writes=[pb(b)], sig=(k == 7))
                    S.tt("dve", f3T[:], bankb(b)[:, 0:1024].rearrange("p (k t) -> p k t", k=8),
                         gfm[:, 2, :].unsqueeze(2).to_broadcast([128, 8, 128]), ALU.mult, reads=[pb(b), "gfm"], writes=["f3T"])
                    b = gp()
                    for k in range(2):
                        S.tr(bankb(b)[:, k * 128:(k + 1) * 128], p_bf[sl][:, k * 128:(k + 1) * 128], idb[:],
                             reads=[Pp, "idb"], writes=[pb(b)], sig=(k == 1))
                    S.cp("act", pT[:].rearrange("p k t -> p (k t)"), bankb(b)[:, 0:256], reads=[pb(b)], writes=["pT"])
                    yield
                    for n_ in range(2):
                        b = gp()
                        for k in range(8):
                            S.mm(bank(b), f3T[:, k, :], w_pg_sb[:, k, n_ * 512:(n_ + 1) * 512], k == 0, k == 7,
                                 reads=["f3T", "w_pg"], writes=[pb(b)])
                        S.act(sg3h[:], bank(b), AF.Sigmoid, reads=[pb(b)], writes=["sg3h"])
                        b2 = 6 + n_
                        for k in range(2):
                            S.mm(bank(b2), pT[:, k, :], w_pe_sb[:, k, n_ * 512:(n_ + 1) * 512], k == 0, k == 1,
                                 reads=["pT", "w_pe"], writes=[pb(b2)])
                        S.tt("dve", yt[:, n_ * 512:(n_ + 1) * 512], bank(b2), sg3h[:], ALU.mult,
                             reads=[pb(b2), "sg3h"], writes=[("yt", n_)])
                        S.tt("pool", yt[:, n_ * 512:(n_ + 1) * 512], yt[:, n_ * 512:(n_ + 1) * 512], ht[sl][:, n_ * 512:(n_ + 1) * 512],
                             ALU.add, reads=[("yt", n_), H], writes=[("yt", n_)])
                        yield
                    S.dma("sp", y_out, yt[:], "yo", reads=[("yt", 0), ("yt", 1)])

                tl = ([("s", 0, 0)] if do_sample else []) + [("p", s_, i_) for s_ in range(2) for i_ in range(ntp)]
                tl = [t + (k,) for k, t in enumerate(tl)]
                prev = None
                if tl:
                    p2_load(*tl[0])
                    p2_front_a(*tl[0])
                    p2_front_b(*tl[0])
                for j, t in enumerate(tl):
                    if j + 1 < len(tl):
                        after = (lambda j=j: p2_load(*tl[j + 1]))
                    else:
                        after = (lambda: None)
                    gmid = p2_mid(*t, prev, after)
                    next(gmid)
                    if j + 1 < len(tl):
                        p2_front_a(*tl[j + 1])
                    for _ in gmid:
                        pass
                    if j + 1 < len(tl):
                        p2_front_b(*tl[j + 1])
                    prev = p2_back(*t)
                if prev is not None:
                    for _ in prev:
                        pass
                S.emit()
        else:
            S.emit()
    return nc


def _t5_bucket(rel):
    rel = np.asarray(rel, np.int64)
    nb, me = 16, 8
    ret = np.where(rel > 0, nb, 0)
    n = np.abs(rel)
    nf = np.maximum(n, 1).astype(np.float32)
    large = me + (np.log(nf / np.float32(me)).astype(np.float32) / np.float32(math.log(128 / me))
                  * np.float32(nb - me)).astype(np.int32)
    large = np.minimum(large, nb - 1)
    return ret + np.where(n < me, n, large)


def _consts():
    c = np.zeros((128, NCST), np.float32)
    c2 = np.zeros((128, NC2), np.float32)
    c2[:, C2_ID:C2_ID + 128] = np.eye(128, dtype=np.float32)
    key = np.arange(128)[:, None]
    q = np.arange(128)[None, :]
    c2[:, C2_M0:C2_M0 + 128] = np.where((key // 64) > (q // 64), NEG, 0.0)
    c2[:, C2_MN:C2_MN + 128] = np.where((key // 32) != (q // 32), NEG, 0.0)
    c[:, C_RMP:C_RMP + 128] = (key <= q).astype(np.float32)
    c[:, C_RMS:C_RMS + 128] = ((key // 32 == q // 32) & (key <= q)).astype(np.float32)
    for b in range(4):
        c2[:, C2_SMQ + b * 128:C2_SMQ + (b + 1) * 128] = (q // 32 == b).astype(np.float32)
        c[:, C_KM + b] = (np.arange(128) // 32 == b).astype(np.float32)
    gam = 1.0 - np.power(2.0, -5.0 - np.arange(4, dtype=np.float64))
    c[:, C_GAM:C_GAM + 4] = (gam ** 128)[None, :]
    c[:, C_GAM + 4:C_GAM + 8] = (gam ** 32)[None, :]
    for kind, n in ((0, np.arange(128)), (1, np.arange(128) % 32)):
        for h in range(4):
            c[:, C_DQK + kind * 8 + h] = gam[h] ** (n + 1.0)
            c[:, C_DQK + kind * 8 + 4 + h] = gam[h] ** (-(n + 1.0)) * (128 ** -0.5)
    return c, c2


def _rot_tables():
    half = 64
    inv = np.power(np.float32(10000.0), -np.arange(half, dtype=np.float32) / np.float32(half)).astype(np.float32)
    out = np.zeros((17, 128, 2, 128), np.float32)
    for t in range(17):
        if t < 16:
            pos = t * 128 + np.arange(128)
        else:
            pos = 4096 + (np.arange(128) % 32)
        ang = (pos[:, None].astype(np.float32) * inv[None, :]).astype(np.float32)
        cs, sn = np.cos(ang.astype(np.float64)), np.sin(ang.astype(np.float64))
        out[t, :, 0, :] = np.concatenate([cs, cs], 1)
        out[t, :, 1, :] = np.concatenate([-sn, sn], 1)
    return out


_NC_CACHE = {}
import os as _os
_SKIP = _os.environ.get("KSKIP", "").split(",")


def kernel(x_prompt, x_sample, p_prompt, p_sample, cache_k, cache_v, state_ret, state_conv,
           rel_bias, g_mix, w_in, g_q, g_k, lam_q1, lam_k1, lam_q2, lam_k2, g_da, g_rt,
           w_bd, w_br, w_o, g_ffn, w_g, w_u, conv_w, conv_b, w_d, g_pe, w_pe, w_pg, _ntp=16, _samp=True, _p2=True):
    f = lambda a: np.ascontiguousarray(np.asarray(a, np.float32))
    x_prompt, x_sample, p_prompt, p_sample = f(x_prompt), f(x_sample), f(p_prompt)[0], f(p_sample)[0]
    cache_k, cache_v, state_ret, state_conv = f(cache_k)[0], f(cache_v)[0], f(state_ret)[0], f(state_conv)[0]
    rel_bias = f(rel_bias)
    rep = lambda v: np.broadcast_to(f(v).reshape(1, -1), (128, f(v).size))
    gfmh = np.ascontiguousarray(np.stack([f(g_mix).reshape(8, 128).T, f(g_ffn).reshape(8, 128).T,
                                          f(g_pe).reshape(8, 128).T], 1))
    gsm = np.ascontiguousarray(np.concatenate(
        [rep(g_q), rep(g_k), rep(lam_q1), rep(lam_k1), rep(lam_q2), rep(lam_k2), rep(g_da), rep(g_rt)], 1))
    cwt = np.concatenate([f(conv_w)[0], f(conv_b)], 0)
    cw = np.ascontiguousarray(cwt.reshape(4, NCC, 128).transpose(2, 0, 1))
    key = np.arange(128)[:, None]
    q = np.arange(128)[None, :]
    bk = np.stack([_t5_bucket(key - q), _t5_bucket(key - 128 - q), _t5_bucket((key % 32) - (q % 32))], 0)
    DRh = np.ascontiguousarray(rel_bias[bk].transpose(1, 0, 3, 2))
    CHh = np.ascontiguousarray(np.broadcast_to(rel_bias[15][None, :], (128, 4)))
    cst, cst2 = _consts()
    rot = _rot_tables()
    shared = dict(w_in=f(w_in)[0], w_bd=f(w_bd)[0], w_br=f(w_br)[0], w_o=f(w_o)[0], w_g=f(w_g)[0], w_u=f(w_u)[0],
                  w_d=f(w_d)[0], w_pe=f(w_pe)[0], w_pg=f(w_pg)[0], gfm=gfmh, gsm=gsm, cw=cw, DR=DRh, CH=CHh,
                  cst=cst, cst2=cst2, rot=rot)
    in_maps = []
    for c in range(8):
        m = dict(shared)
        m["xp"] = x_prompt[2 * c:2 * c + 2]
        m["xs"] = x_sample[4 * c:4 * c + 4].reshape(128, D)
        m["pp"] = p_prompt[2 * c:2 * c + 2]
        m["psm"] = p_sample[4 * c:4 * c + 4].reshape(128, 256)
        m["ck"] = cache_k[4 * c:4 * c + 4].reshape(4, 4096, 512)
        m["cv"] = cache_v[4 * c:4 * c + 4].reshape(4, 4096, 512)
        m["sret"] = state_ret[4 * c:4 * c + 4]
        sc = state_conv[4 * c:4 * c + 4]
        m["scv"] = np.ascontiguousarray(sc.reshape(4, 2, NCC, 128).transpose(3, 2, 0, 1))
        in_maps.append(m)
    ck_ = (_ntp, _samp, _p2)
    if ck_ not in _NC_CACHE:
        _NC_CACHE[ck_] = build(ntp=_ntp, do_sample=_samp, do_p2=_p2)
    ncr = int(_os.environ.get("KNCORES", "8"))
    res = run_bass_kernel_spmd(_NC_CACHE[ck_], in_maps[:ncr], core_ids=list(range(ncr)))
    R = list(res.results) + [res.results[0]] * (8 - ncr)
    cat = lambda k: np.concatenate([r[k] for r in R], 0)
    y_p = cat("yp")
    y_s = cat("ys").reshape(32, 32, D)
    k_p = cat("kp").reshape(1, 16, 2048, 4, 128)
    v_p = cat("vp").reshape(1, 16, 2048, 4, 128)
    r_p = cat("rp").reshape(1, 16, 4, 128, 128)
    c_p = cat("cpo").reshape(1, 16, 2, DFF)
    k_s = cat("kso").reshape(1, 32, 32, 4, 128)
    v_s = cat("vso").reshape(1, 32, 32, 4, 128)
    r_s = cat("rso").reshape(1, 32, 4, 128, 128)
    c_s = cat("cso").reshape(1, 32, 2, DFF)
    return (y_p, y_s, k_p, v_p, r_p, c_p, k_s, v_s, r_s, c_s)
```
